# Optimizing a Trainium2 kernel written in Bass

```python
import math
import jax, jax.numpy as jnp
from jax import lax
import numpy as np

D_MODEL = 2048
BATCH = 16
SEQ = 2048
DEPTH = 2

GRID_W = 64
CTX_LEN = 256
FFN_HIDDEN = -(-8 * D_MODEL // (3 * 256)) * 256
MIX_WIDTH = D_MODEL
POOL_GROUPS = 4
POOL_WINDOWS = (2, 4, 8, 16)
POOL_WIDTH = MIX_WIDTH // 2
POOL_GROUP_DIM = POOL_WIDTH // POOL_GROUPS
FOURIER_GROUPS = 4
FOURIER_WIDTH = MIX_WIDTH - POOL_WIDTH
FOURIER_GROUP_DIM = FOURIER_WIDTH // FOURIER_GROUPS
DIFF_HEAD_DIM = 128
DIFF_HEADS = D_MODEL // (2 * DIFF_HEAD_DIM)
DIFF_QK_WIDTH = DIFF_HEADS * 2 * DIFF_HEAD_DIM
DIFF_V_WIDTH = DIFF_HEADS * 2 * DIFF_HEAD_DIM
ROPE_THETA = 10000.0
Q_BLOCK = 128
N_EVEN = (DEPTH + 1) // 2
N_ODD = DEPTH // 2
DEEPNORM_ALPHA = (2.0 * DEPTH) ** 0.25
DEEPNORM_BETA = (8.0 * DEPTH) ** -0.25
LN_EPS = 1e-5
RMS_EPS = 1e-5

kernel_name = 'hybrid_pool_fourier_diffattn_dit'


def layer_norm(x, g, b):
    xf = x.astype(jnp.float32)
    mu = jnp.mean(xf, axis=-1, keepdims=True)
    var = jnp.mean(jnp.square(xf - mu), axis=-1, keepdims=True)
    y = (xf - mu) * lax.rsqrt(var + LN_EPS) * g.astype(jnp.float32) + b.astype(jnp.float32)
    return y.astype(x.dtype)


def modulate(x, shift, scale):
    return x * (1.0 + scale) + shift


def swiglu(h, w_gate, w_up, w_down):
    return (jax.nn.silu(h @ w_gate) * (h @ w_up)) @ w_down


def pool_mixer(u, lin, scale):
    b_, n, _ = u.shape
    ug = u.reshape(b_, n, POOL_GROUPS, POOL_GROUP_DIM)
    cs = jnp.cumsum(ug.astype(jnp.float32), axis=1)
    cs = jnp.pad(cs, ((0, 0), (1, 0), (0, 0), (0, 0)))
    t = jnp.arange(n)
    outs = []
    for gi, w in enumerate(POOL_WINDOWS):
        pad = w // 2
        lo_off = -(w // 2)
        hi_off = w - w // 2
        csp = jnp.pad(cs[:, :, gi], ((0, 0), (pad, pad), (0, 0)), mode='edge')
        total = csp[:, hi_off + pad:hi_off + pad + n] - csp[:, lo_off + pad:lo_off + pad + n]
        count = (jnp.minimum(t + hi_off, n) - jnp.maximum(t + lo_off, 0)).astype(jnp.float32)
        outs.append(total / count[None, :, None] - ug[:, :, gi].astype(jnp.float32))
    pooled = jnp.stack(outs, axis=2).astype(u.dtype)
    y = jnp.einsum('bngc,gcd->bngd', pooled, lin)
    return y.reshape(b_, n, POOL_WIDTH) * scale


def fourier_mixer(u, lin):
    b_, n, _ = u.shape
    ug = u.reshape(b_, n, FOURIER_GROUPS, FOURIER_GROUP_DIM)
    f = jnp.fft.fft2(ug.astype(jnp.float32), axes=(1, 3), norm='ortho').real.astype(u.dtype)
    y = jnp.einsum('bngc,gcd->bngd', f, lin)
    return y.reshape(b_, n, FOURIER_WIDTH)


def pool_fourier_layer_mixer(h, w_in, pool_lin, pool_scale, fourier_lin, w_out):
    u = h @ w_in
    ya = pool_mixer(u[..., :POOL_WIDTH], pool_lin, pool_scale)
    yb = fourier_mixer(u[..., POOL_WIDTH:], fourier_lin)
    return jnp.concatenate([ya, yb], axis=-1) @ w_out


def axial_rope_tables(rows):
    row = jnp.repeat(jnp.arange(rows), GRID_W).astype(jnp.float32)
    col = jnp.tile(jnp.arange(GRID_W), rows).astype(jnp.float32)
    axis_dim = DIFF_HEAD_DIM // 2
    inv_freq = ROPE_THETA ** (-jnp.arange(0, axis_dim, 2, dtype=jnp.float32) / axis_dim)
    ang_r = row[:, None] * inv_freq[None, :]
    ang_c = col[:, None] * inv_freq[None, :]
    return (jnp.cos(ang_r), jnp.sin(ang_r), jnp.cos(ang_c), jnp.sin(ang_c))


def rope_rotate(x, cos, sin):
    x1, x2 = jnp.split(x, 2, axis=-1)
    return jnp.concatenate([x1 * cos - x2 * sin, x2 * cos + x1 * sin], axis=-1)


def apply_axial_rope(x, cos_r, sin_r, cos_c, sin_c):
    cr, sr, cc, sc = [t[None, :, None, None, :].astype(x.dtype) for t in (cos_r, sin_r, cos_c, sin_c)]
    half = DIFF_HEAD_DIM // 2
    return jnp.concatenate([rope_rotate(x[..., :half], cr, sr), rope_rotate(x[..., half:], cc, sc)], axis=-1)


def diff_attention(h_lat, h_ctx, rope_tabs, w_in, lam_q1, lam_k1, lam_q2, lam_k2, subln_g, w_out, lambda_init, need_ctx):
    def project(h):
        b_, n, _ = h.shape
        qkv = h @ w_in
        q = qkv[..., :DIFF_QK_WIDTH].reshape(b_, n, DIFF_HEADS, 2, DIFF_HEAD_DIM)
        k = qkv[..., DIFF_QK_WIDTH:2 * DIFF_QK_WIDTH].reshape(b_, n, DIFF_HEADS, 2, DIFF_HEAD_DIM)
        v = qkv[..., 2 * DIFF_QK_WIDTH:].reshape(b_, n, DIFF_HEADS, 2 * DIFF_HEAD_DIM)
        return q, k, v

    q_l, k_l, v_l = project(h_lat)
    q_c, k_c, v_c = project(h_ctx)
    q_l = apply_axial_rope(q_l, *rope_tabs)
    k_l = apply_axial_rope(k_l, *rope_tabs)
    lam = (jnp.exp(jnp.sum(lam_q1.astype(jnp.float32) * lam_k1.astype(jnp.float32)))
           - jnp.exp(jnp.sum(lam_q2.astype(jnp.float32) * lam_k2.astype(jnp.float32))) + lambda_init)
    sm_scale = 1.0 / math.sqrt(DIFF_HEAD_DIM)

    def attend(q, k, v):
        s = jnp.einsum('bqhmd,bkhmd->bhmqk', q, k).astype(jnp.float32) * sm_scale
        p = jax.nn.softmax(s, axis=-1)
        wts = (p[:, :, 0] - lam * p[:, :, 1]).astype(v.dtype)
        o = jnp.einsum('bhqk,bkhe->bqhe', wts, v).astype(jnp.float32)
        o = o * lax.rsqrt(jnp.mean(jnp.square(o), axis=-1, keepdims=True) + RMS_EPS)
        o = o * subln_g.astype(jnp.float32) * (1.0 - lambda_init)
        return o.reshape(o.shape[0], o.shape[1], DIFF_V_WIDTH).astype(v.dtype)

    b_, n = h_lat.shape[0], h_lat.shape[1]
    k_all = jnp.concatenate([k_c, k_l], axis=1)
    v_all = jnp.concatenate([v_c, v_l], axis=1)
    n_blocks = n // Q_BLOCK
    q_blocks = q_l.reshape(b_, n_blocks, Q_BLOCK, DIFF_HEADS, 2, DIFF_HEAD_DIM).swapaxes(0, 1)
    o_blocks = lax.map(lambda qb: attend(qb, k_all, v_all), q_blocks)
    o_lat = o_blocks.swapaxes(0, 1).reshape(b_, n, DIFF_V_WIDTH) @ w_out
    o_ctx = attend(q_c, k_c, v_c) @ w_out if need_ctx else None
    return o_lat, o_ctx


def setup_inputs(seed: int = 0) -> dict:
    key = jax.random.key(seed)
    ks = jax.random.split(key, 32)
    f32 = jnp.float32

    def nrm(k, shape, scale):
        return jax.random.normal(k, shape, f32) * scale

    D = D_MODEL
    F = FFN_HIDDEN
    return {
        'x': nrm(ks[0], (BATCH, SEQ, D), 1.0),
        'c': nrm(ks[1], (BATCH, D), 1.0),
        'ctx': nrm(ks[2], (BATCH, CTX_LEN, D), 1.0),
        'c_ctx': nrm(ks[3], (D,), 1.0),
        'w_mod': nrm(ks[4], (DEPTH, D, 6 * D), D ** -0.5),
        'b_mod': nrm(ks[5], (DEPTH, 6 * D), 0.02),
        'ln1_g': 1.0 + nrm(ks[6], (DEPTH, D), 0.05),
        'ln1_b': nrm(ks[7], (DEPTH, D), 0.02),
        'ln2_g': 1.0 + nrm(ks[8], (DEPTH, D), 0.05),
        'ln2_b': nrm(ks[9], (DEPTH, D), 0.02),
        'ffn_w_gate': nrm(ks[10], (DEPTH, D, F), D ** -0.5),
        'ffn_w_up': nrm(ks[11], (DEPTH, D, F), D ** -0.5),
        'ffn_w_down': nrm(ks[12], (DEPTH, F, D), F ** -0.5 * DEEPNORM_BETA),
        'pf_w_in': nrm(ks[13], (N_EVEN, D, MIX_WIDTH), D ** -0.5),
        'pool_lin': nrm(ks[14], (N_EVEN, POOL_GROUPS, POOL_GROUP_DIM, POOL_GROUP_DIM), POOL_GROUP_DIM ** -0.5),
        'pool_scale': 1.0 + nrm(ks[15], (N_EVEN, POOL_WIDTH), 0.05),
        'fourier_lin': nrm(ks[16], (N_EVEN, FOURIER_GROUPS, FOURIER_GROUP_DIM, FOURIER_GROUP_DIM), FOURIER_GROUP_DIM ** -0.5),
        'pf_w_out': nrm(ks[17], (N_EVEN, MIX_WIDTH, D), MIX_WIDTH ** -0.5 * DEEPNORM_BETA),
        'da_w_in': nrm(ks[18], (N_ODD, D, 2 * DIFF_QK_WIDTH + DIFF_V_WIDTH), D ** -0.5),
        'da_lam_q1': nrm(ks[19], (N_ODD, DIFF_HEAD_DIM), 0.1),
        'da_lam_k1': nrm(ks[20], (N_ODD, DIFF_HEAD_DIM), 0.1),
        'da_lam_q2': nrm(ks[21], (N_ODD, DIFF_HEAD_DIM), 0.1),
        'da_lam_k2': nrm(ks[22], (N_ODD, DIFF_HEAD_DIM), 0.1),
        'da_subln_g': 1.0 + nrm(ks[23], (N_ODD, 2 * DIFF_HEAD_DIM), 0.05),
        'da_w_out': nrm(ks[24], (N_ODD, DIFF_V_WIDTH, D), DIFF_V_WIDTH ** -0.5 * DEEPNORM_BETA),
    }


def reference(x, c, ctx, c_ctx, w_mod, b_mod, ln1_g, ln1_b, ln2_g, ln2_b, ffn_w_gate, ffn_w_up, ffn_w_down,
              pf_w_in, pool_lin, pool_scale, fourier_lin, pf_w_out,
              da_w_in, da_lam_q1, da_lam_k1, da_lam_q2, da_lam_k2, da_subln_g, da_w_out):
    rows = x.shape[1] // GRID_W
    rope_tabs = axial_rope_tables(rows)
    silu_c = jax.nn.silu(c)
    silu_cc = jax.nn.silu(c_ctx)
    for l in range(DEPTH):
        last = l == DEPTH - 1
        j = l // 2
        mod = (silu_c @ w_mod[l] + b_mod[l])[:, None, :]
        mod_c = silu_cc @ w_mod[l] + b_mod[l]
        sh1, sc1, g1, sh2, sc2, g2 = jnp.split(mod, 6, axis=-1)
        csh1, csc1, cg1, csh2, csc2, cg2 = jnp.split(mod_c, 6, axis=-1)
        hx = modulate(x, sh1, sc1)
        hc = modulate(ctx, csh1, csc1)
        if l % 2 == 0:
            mx = pool_fourier_layer_mixer(hx, pf_w_in[j], pool_lin[j], pool_scale[j], fourier_lin[j], pf_w_out[j])
            mc = None if last else pool_fourier_layer_mixer(hc, pf_w_in[j], pool_lin[j], pool_scale[j], fourier_lin[j], pf_w_out[j])
        else:
            lambda_init = 0.8 - 0.6 * math.exp(-0.3 * l)
            mx, mc = diff_attention(hx, hc, rope_tabs, da_w_in[j], da_lam_q1[j], da_lam_k1[j], da_lam_q2[j],
                                    da_lam_k2[j], da_subln_g[j], da_w_out[j], lambda_init, not last)
        x = layer_norm(DEEPNORM_ALPHA * x + g1 * mx, ln1_g[l], ln1_b[l])
        fx = swiglu(modulate(x, sh2, sc2), ffn_w_gate[l], ffn_w_up[l], ffn_w_down[l])
        x = layer_norm(DEEPNORM_ALPHA * x + g2 * fx, ln2_g[l], ln2_b[l])
        if not last:
            ctx = layer_norm(DEEPNORM_ALPHA * ctx + cg1 * mc, ln1_g[l], ln1_b[l])
            fc = swiglu(modulate(ctx, csh2, csc2), ffn_w_gate[l], ffn_w_up[l], ffn_w_down[l])
            ctx = layer_norm(DEEPNORM_ALPHA * ctx + cg2 * fc, ln2_g[l], ln2_b[l])
    return x
```

```python
import os
import math
import numpy as np
import ml_dtypes
import concourse.bass as bass
import concourse.mybir as mybir
from concourse.bass_utils import run_bass_kernel_spmd
from contextlib import ExitStack

F32 = mybir.dt.float32
BF16 = mybir.dt.bfloat16
F32R = mybir.dt.float32r
AF = mybir.ActivationFunctionType
ALU = mybir.AluOpType
AX = mybir.AxisListType

D = 2048
NTOK = 2048
NCTX = 256
FF = 5632
NS = 2
KC = 16
FC = 44
ALPHA = (2.0 * 2) ** 0.25
LN_EPS = 1e-5
RMS_EPS = 1e-5
LAMBDA_INIT1 = 0.8 - 0.6 * math.exp(-0.3 * 1)
SM_SCALE = 1.0 / math.sqrt(128.0)
POOL_WINDOWS = (2, 4, 8, 16)

DEBUG = bool(int(os.environ.get("KDEBUG", "0")))
STOP_AFTER = os.environ.get("KSTOP", "")
DUMPN = os.environ.get("KDUMPN", "").split(",")
NOPOOL = bool(int(os.environ.get("KNOPOOL", "1")))
DUMPS = os.environ.get("KDUMP", "").split(",")


class Buf:
    __slots__ = ("name", "w", "r", "excl")

    def __init__(self, name, excl=False):
        self.name = name
        self.w = None
        self.r = {}
        self.excl = excl


class Sync:
    def __init__(self, nc, stack, n_dma=28):
        self.nc = nc
        self.engs = {"pe": nc.tensor, "act": nc.scalar, "dve": nc.vector, "pool": nc.gpsimd, "sp": nc.sync}
        self.sems = {}
        self.cnt = {}
        for e in self.engs:
            self.sems[e] = stack.enter_context(nc.semaphore("s_" + e))
            self.cnt[e] = 0
        self.n_dma = n_dma
        for i in range(n_dma):
            self.sems[("d", i)] = stack.enter_context(nc.semaphore("s_d%d" % i))
            self.cnt[("d", i)] = 0
        self.rr = 0
        self.known = {e: {} for e in self.engs}
        self.snap = {}
        self.nwaits = 0
        self.ninstr = 0

    def _learn(self, E, ev):
        k = dict(self.known[E])
        sn = self.snap.get(ev)
        if sn:
            for s, v in sn.items():
                if k.get(s, 0) < v:
                    k[s] = v
        if k.get(ev[0], 0) < ev[1]:
            k[ev[0]] = ev[1]
        self.known[E] = k

    def _wait(self, E, s, v):
        if s == E and E in ("pe", "sp"):
            return
        if self.known[E].get(s, 0) >= v:
            return
        self.engs[E].wait_ge(self.sems[s], v)
        self.nwaits += 1
        self._learn(E, (s, v))

    def sync(self, E, reads=(), writes=()):
        need = {}
        for b in reads:
            if b.w is not None:
                s, v = b.w
                if need.get(s, 0) < v:
                    need[s] = v
            if b.excl:
                for s, v in b.r.items():
                    if need.get(s, 0) < v:
                        need[s] = v
        for b in writes:
            if b.w is not None:
                s, v = b.w
                if need.get(s, 0) < v:
                    need[s] = v
            for s, v in b.r.items():
                if need.get(s, 0) < v:
                    need[s] = v
        for s, v in need.items():
            self._wait(E, s, v)

    def done(self, E, ins, reads=(), writes=()):
        self.cnt[E] += 1
        ins.then_inc(self.sems[E], 1)
        ev = (E, self.cnt[E])
        self.snap[ev] = self.known[E]
        for b in writes:
            b.w = ev
            b.r = {}
        for b in reads:
            if b.r.get(E, 0) < ev[1]:
                b.r[E] = ev[1]
        self.ninstr += 1

    def op(self, E, fn, reads=(), writes=()):
        if E == "pool" and NOPOOL:
            E = "dve"
        self.sync(E, reads, writes)
        ins = fn(self.engs[E])
        self.done(E, ins, reads, writes)
        return ins

    def dma(self, out, in_, reads=(), writes=(), Q="sp"):
        i = self.rr
        self.rr = (i + 1) % self.n_dma
        key = ("d", i)
        prev = self.cnt[key]
        if prev:
            self._wait(Q, key, prev)
        self.sync(Q, reads, writes)
        ins = self.engs[Q].dma_start(out=out, in_=in_)
        ins.then_inc(self.sems[key], 16)
        self.cnt[key] += 16
        ev = (key, self.cnt[key])
        self.snap[ev] = self.known[Q]
        for b in writes:
            b.w = ev
            b.r = {}
        for b in reads:
            if b.r.get(key, 0) < ev[1]:
                b.r[key] = ev[1]
        self.ninstr += 1

    def barrier(self, engines=("pe", "act", "dve", "pool", "sp")):
        for E in engines:
            for s, v in self.cnt.items():
                if v:
                    self._wait(E, s, v)


class Stream:
    def __init__(self, S, tiles, loads, ahead=None):
        self.S = S
        self.tiles = tiles
        self.loads = loads
        self.issued = 0
        self.n = len(loads)
        self.ahead = (len(tiles) - 1) if ahead is None else ahead

    def _issue(self, i):
        t, b = self.tiles[i % len(self.tiles)]
        for (o, a, dbufs) in self.loads[i](t):
            self.S.dma(o, a, reads=dbufs, writes=[b])

    def get(self, i):
        ahead = self.ahead
        while self.issued < min(self.n, i + 1 + ahead):
            self._issue(self.issued)
            self.issued += 1
        return self.tiles[i % len(self.tiles)]


def _host_consts():
    c = {}
    c["ident"] = np.eye(128, dtype=np.float32)
    c["identb"] = np.eye(128, dtype=np.float32).astype(ml_dtypes.bfloat16)
    n = np.arange(256)
    ang = 2 * np.pi * np.outer(n, n) / 256.0
    cs = np.concatenate([np.cos(ang), np.sin(ang)], axis=1) / 16.0
    c["dft_c"] = cs.reshape(2, 128, 512).transpose(1, 0, 2).astype(ml_dtypes.bfloat16).copy()
    for N, nm in ((2048, "dft_n"), (256, "dft_x")):
        k = np.arange(N, dtype=np.int64)
        ang = 2 * np.pi * ((np.outer(k, k) % N).astype(np.float64)) / N
        sc = 1.0 / math.sqrt(N)
        Cn = (np.cos(ang) * sc)
        Sn = (-np.sin(ang) * sc)
        kt = min(512, N)
        nch = N // 128
        arr = np.stack([Cn, Sn], axis=0)
        arr = arr.reshape(2, nch, 128, N // kt, kt)
        arr = arr.transpose(3, 2, 1, 0, 4)
        c[nm] = np.ascontiguousarray(arr).astype(ml_dtypes.bfloat16)
    c["dft_n"] = c["dft_n"].reshape(4, 128, -1)
    c["dft_x"] = c["dft_x"].reshape(1, 128, -1)
    for N, nm in ((2048, "efac_n"), (256, "efac_x")):
        t = np.arange(N)
        rows = []
        for w in POOL_WINDOWS:
            lo = -(w // 2)
            hi = w - w // 2
            cnt = np.minimum(t + hi, N) - np.maximum(t + lo, 0)
            f = w / cnt
            rows.append(np.concatenate([f[:8], f[-8:]]))
        c[nm] = np.stack(rows).astype(np.float32)
    rows_ = np.repeat(np.arange(32), 64).astype(np.float32)
    cols_ = np.tile(np.arange(64), 32).astype(np.float32)
    inv = (np.float32(10000.0) ** (-np.arange(0, 64, 2, dtype=np.float32) / np.float32(64.0))).astype(np.float32)
    ang_r = (rows_[:, None] * inv[None, :]).astype(np.float32)
    ang_c = (cols_[:, None] * inv[None, :]).astype(np.float32)
    c["rope_cos"] = np.concatenate([np.cos(ang_r), np.cos(ang_r), np.cos(ang_c), np.cos(ang_c)], axis=1).astype(np.float32)
    c["rope_sin"] = np.concatenate([-np.sin(ang_r), np.sin(ang_r), -np.sin(ang_c), np.sin(ang_c)], axis=1).astype(np.float32)
    return c


CONST_SPECS = None


class MK:
    def __init__(self):
        self.nc = bass.Bass("TRN2", target_bir_lowering=False)
        self.stack = ExitStack()
        self.dbg = {}

    def din(self, name, shape, dt=F32):
        return self.nc.dram_tensor(name, list(shape), dt, kind="ExternalInput").ap()

    def dscratch(self, name, shape, dt):
        return self.nc.dram_tensor(name, list(shape), dt).ap()

    def dout(self, name, shape, dt=F32):
        return self.nc.dram_tensor(name, list(shape), dt, kind="ExternalOutput").ap()

    def sb(self, st, name, shape, dt):
        t = st.enter_context(self.nc.sbuf_tensor(name, list(shape), dt))
        return t

    def build(self, consts):
        nc = self.nc
        st = self.stack
        S = self.S = Sync(nc, st)
        self.uid = 0
        I = self.I = {}
        I["x"] = self.din("x", [NS, NTOK, D])
        I["c"] = self.din("c", [NS, D])
        I["ctx"] = self.din("ctx", [NS, NCTX, D])
        I["c_ctx"] = self.din("c_ctx", [D])
        I["w_mod"] = self.din("w_mod", [2, D, 6 * D])
        I["b_mod"] = self.din("b_mod", [2, 6 * D])
        for nm in ("ln1_g", "ln1_b", "ln2_g", "ln2_b"):
            I[nm] = self.din(nm, [2, D])
        I["ffn_w_gate"] = self.din("ffn_w_gate", [2, D, FF])
        I["ffn_w_up"] = self.din("ffn_w_up", [2, D, FF])
        I["ffn_w_down"] = self.din("ffn_w_down", [2, FF, D])
        I["pf_w_in"] = self.din("pf_w_in", [1, D, D])
        I["pool_lin"] = self.din("pool_lin", [1, 4, 256, 256])
        I["pool_scale"] = self.din("pool_scale", [1, 1024])
        I["fourier_lin"] = self.din("fourier_lin", [1, 4, 256, 256])
        I["pf_w_out"] = self.din("pf_w_out", [1, D, D])
        I["da_w_in"] = self.din("da_w_in", [1, D, 3 * D])
        for nm in ("da_lam_q1", "da_lam_k1", "da_lam_q2", "da_lam_k2"):
            I[nm] = self.din(nm, [1, 128])
        I["da_subln_g"] = self.din("da_subln_g", [1, 256])
        I["da_w_out"] = self.din("da_w_out", [1, D, D])
        Cst = self.C = {}
        for k, v in consts.items():
            dt = BF16 if v.dtype == ml_dtypes.bfloat16 else F32
            Cst[k] = self.din("k_" + k, v.shape, dt)
        self.out = self.dout("out", [NS, NTOK, D])

        W = self.W = {}
        W["win"] = self.dscratch("w_win", [4, 128, 16 * 512], BF16)
        W["wout0"] = self.dscratch("w_wout0", [4, 128, 16 * 512], BF16)
        W["wout1"] = self.dscratch("w_wout1", [4, 128, 16 * 512], BF16)
        W["plin"] = self.dscratch("w_plin", [4, 128, 2 * 256], BF16)
        W["flin"] = self.dscratch("w_flin", [4, 128, 2 * 256], BF16)
        W["wg"] = self.dscratch("w_wg", [2, 22, 128, 16 * 256], BF16)
        W["wu"] = self.dscratch("w_wu", [2, 22, 128, 16 * 256], BF16)
        W["wd"] = self.dscratch("w_wd", [2, 16, 128, 44 * 128], BF16)
        W["wa"] = self.dscratch("w_wa", [8, 3, 128, 16 * 256], BF16)
        self.Wb = {}
        NT = NS * 5
        self.XT = [self.dscratch("xt%d" % i, [NT, 128, 16 * 512], F32) for i in range(4)]
        self.HT = [self.dscratch("ht%d" % i, [NT, 128, 16 * 512], BF16) for i in range(4)]
        self.YT = [self.dscratch("yt%d" % i, [NT, 128, 16 * 512], BF16) for i in range(2)]
        self.PQ = self.dscratch("pq", [NS, 18, 128, 4 * 512], BF16)
        self.XTb = [[Buf("xt%d_%d" % (i, t)) for t in range(NT)] for i in range(4)]
        self.HTb = [[Buf("ht%d_%d" % (i, t)) for t in range(NT)] for i in range(4)]
        self.YTb = [[Buf("yt%d_%d" % (i, t)) for t in range(NT)] for i in range(2)]
        self.PQb = [[Buf("pq%d_%d" % (s, t)) for t in range(18)] for s in range(NS)]

        self.ident = self.sb(st, "ident", [128, 128], F32)
        self.identb = self.sb(st, "identb", [128, 128], BF16)
        self.ones_f = self.sb(st, "ones_f", [128, 128], F32)
        self.ones_b = self.sb(st, "ones_b", [128, 128], BF16)
        self.modT = [self.sb(st, "modT%d" % l, [128, 96, 3], F32) for l in range(2)]
        self.vecT = self.sb(st, "vecT", [128, 9, 16], F32)
        self.gb = Buf("globals")
        self.ps = []
        self.psb = []
        for i in range(8):
            t = st.enter_context(nc.psum_tensor("ps%d" % i, [128, 512], F32))
            self.ps.append(t)
            self.psb.append(Buf("ps%d" % i, excl=True))
        self.ps_rr = 0
        self.nbanks = 8
        self.outb = Buf('out')

        S.dma(self.ident[:], self.C["ident"][:, :], writes=[self.gb])
        S.dma(self.identb[:], self.C["identb"][:, :], writes=[self.gb])
        S.op("dve", lambda e: e.memset(self.ones_f[:], 1.0 / D), writes=[self.gb])
        S.op("dve", lambda e: e.memset(self.ones_b[:], 1.0), writes=[self.gb])

        self.tiles = []
        for s in range(NS):
            for j in range(4):
                self.tiles.append((s, j, 512))
            self.tiles.append((s, 4, 256))

        lat = [s * 5 + j for s in range(NS) for j in range(4)]
        allt = list(range(NS * 5))
        steps = [
            ("mod", lambda: self.phase_mod()),
            ("convert", lambda: self.phase_convert()),
            ("xin", lambda: self.phase_xin()),
            ("m1a", lambda: self.phase_m1a()),
            ("m1b", lambda: self.phase_m1b()),
            ("m3", lambda: self.phase_m3()),
            ("proj0", lambda: self.phase_proj_ep("wout0", self.W["wout0"], 0, 0, 0, 1, 1, allt)),
            ("ffn0", lambda: self.phase_ffn(0, 1, 1, 2, 2, allt, False)),
            ("attn", lambda: self.phase_attn()),
            ("proj1", lambda: self.phase_proj_ep("wout1", self.W["wout1"], 1, 1, 2, 3, 3, lat)),
            ("ffn1", lambda: self.phase_ffn(1, 3, 3, None, None, lat, True)),
        ]
        for nm, fn in steps:
            fn()
            if DEBUG and nm in DUMPS:
                self.dump_stage(nm)
            if STOP_AFTER == nm:
                break
        return self.finish()

    def finish(self):
        S = self.S
        S.barrier()
        self.stack.close()
        return self.nc

    def bank(self):
        i = self.ps_rr % self.nbanks
        self.ps_rr = (i + 1) % self.nbanks
        return self.ps[i], self.psb[i]

    def name(self, p):
        self.uid += 1
        return "%s_%d" % (p, self.uid)

    def tile_ap(self, t, tt, ntok, nch=16, full=512):
        return t[tt].rearrange("p (c n) -> p c n", c=nch)[:, :, 0:ntok]

    def phase_mod(self):
        nc, S, I = self.nc, self.S, self.I
        with ExitStack() as st:
            c_sb = self.sb(st, "c_sb", [3, D], F32)
            cs = self.sb(st, "cs", [3, D], F32)
            csT = self.sb(st, "csT", [128, 16, 3], F32)
            bm = self.sb(st, "bm", [3, 6 * D], F32)
            rows = self.sb(st, "modrows", [3, 6 * D], F32)
            vrows = self.sb(st, "vrows", [16, 9, 128], F32)
            wt = [self.sb(st, "wmt%d" % i, [128, 16, 512], F32) for i in range(2)]
            wtb = [Buf("wmt%d" % i) for i in range(2)]
            b_c, b_cs, b_csT, b_bm, b_rows, b_vr = (Buf(n) for n in ("c", "cs", "csT", "bm", "rows", "vr"))
            S.dma(c_sb[0:2, :], I["c"][:, :], writes=[b_c])
            S.dma(c_sb[2:3, :], I["c_ctx"].rearrange("(o d) -> o d", o=1), writes=[b_c])
            vecs = [I["ln1_g"][0], I["ln1_b"][0], I["ln2_g"][0], I["ln2_b"][0],
                    I["ln1_g"][1], I["ln1_b"][1], I["ln2_g"][1], I["ln2_b"][1]]
            S.op("dve", lambda e: e.memset(vrows[:], 0.0), writes=[b_vr])
            for v, ap in enumerate(vecs):
                S.dma(vrows[0:16, v, :], ap.rearrange("(c p) -> c p", p=128), writes=[b_vr])
            S.dma(vrows[0:8, 8, :], I["pool_scale"][0].rearrange("(c p) -> c p", p=128), writes=[b_vr])
            S.op("act", lambda e: e.activation(out=cs[:], in_=c_sb[:], func=AF.Silu), reads=[b_c], writes=[b_cs])
            pt, pb = self.bank()
            S.sync("pe", reads=[b_cs, self.gb], writes=[pb])
            for k in range(16):
                ins = nc.tensor.transpose(out=pt[:, k * 3:(k + 1) * 3], in_=cs[0:3, k * 128:(k + 1) * 128],
                                          identity=self.ident[0:3, 0:3])
            S.done("pe", ins, reads=[b_cs, self.gb], writes=[pb])
            S.op("dve", lambda e: e.tensor_copy(out=csT[:].rearrange("p k s -> p (k s)"), in_=pt[:, 0:48]),
                 reads=[pb], writes=[b_csT])
            pt, pb = self.bank()
            S.sync("pe", reads=[b_vr, self.gb], writes=[pb])
            for v in range(9):
                ins = nc.tensor.transpose(out=pt[:, v * 16:(v + 1) * 16], in_=vrows[0:16, v, :],
                                          identity=self.ident[0:16, 0:16])
            S.done("pe", ins, reads=[b_vr, self.gb], writes=[pb])
            S.op("dve", lambda e: e.tensor_copy(out=self.vecT[:].rearrange("p v c -> p (v c)"), in_=pt[:, 0:144]),
                 reads=[pb], writes=[self.gb])
            for l in range(2):
                S.dma(bm[:], I["b_mod"][l].rearrange("(o d) -> o d", o=1).broadcast_to([3, 6 * D]), writes=[b_bm],
                      reads=[b_rows])
                wsrc = I["w_mod"][l].rearrange("(k p) n -> p k n", p=128)

                def mk(nb):
                    return lambda t: [(t[:], wsrc[:, :, nb * 512:(nb + 1) * 512], [])]
                strm = Stream(S, list(zip(wt, wtb)), [mk(nb) for nb in range(24)])
                for nb in range(24):
                    t, tb = strm.get(nb)
                    pt, pb = self.bank()
                    S.sync("pe", reads=[tb, b_csT], writes=[pb])
                    for k in range(16):
                        ins = nc.tensor.matmul(pt[0:3, :], lhsT=csT[:, k, :], rhs=t[:, k, :],
                                               start=(k == 0), stop=(k == 15))
                    S.done("pe", ins, reads=[tb, b_csT], writes=[pb])
                    S.op("dve", lambda e: e.tensor_tensor(out=rows[0:3, nb * 512:(nb + 1) * 512], in0=pt[0:3, :],
                                                          in1=bm[0:3, nb * 512:(nb + 1) * 512], op=ALU.add),
                         reads=[pb, b_bm], writes=[b_rows])
                pt, pb = self.bank()
                S.sync("pe", reads=[b_rows, self.gb], writes=[pb])
                for blk in range(96):
                    ins = nc.tensor.transpose(out=pt[:, blk * 3:(blk + 1) * 3], in_=rows[0:3, blk * 128:(blk + 1) * 128],
                                              identity=self.ident[0:3, 0:3])
                S.done("pe", ins, reads=[b_rows, self.gb], writes=[pb])
                S.op("dve", lambda e: e.tensor_copy(out=self.modT[l][:].rearrange("p v s -> p (v s)"), in_=pt[:, 0:288]),
                     reads=[pb], writes=[self.gb])
                if DEBUG:
                    d = self.dout("dbg_modrows%d" % l, [3, 6 * D])
                    S.dma(d[:, :], rows[:], reads=[b_rows])
            if DEBUG:
                d = self.dout("dbg_modT", [2, 128, 96 * 3])
                for l in range(2):
                    S.dma(d[l], self.modT[l][:].rearrange("p v s -> p (v s)"), reads=[self.gb])
                d = self.dout("dbg_vecT", [128, 9 * 16])
                S.dma(d[:, :], self.vecT[:].rearrange("p v c -> p (v c)"), reads=[self.gb])
            S.barrier()

    def modv(self, l, v, s):
        return self.modT[l][:, v * 16:(v + 1) * 16, s]

    def dump(self, name, ap, bufs, dt):
        if not DEBUG or name not in DUMPN:
            return
        d = self.dout("dbg_" + name, list(ap.shape), dt)
        if len(ap.shape) == 3:
            for i in range(ap.shape[0]):
                self.S.dma(d[i], ap[i], reads=bufs)
        else:
            self.S.dma(d, ap, reads=bufs)

    def phase_convert(self):
        nc, S, I, W = self.nc, self.S, self.I, self.W
        jobs = []

        def add(src, kc, cw, dst, key):
            jobs.append((src, kc, cw, dst, key))
            self.Wb[key] = Buf("w" + str(key))

        win = I["pf_w_in"][0].rearrange("(k p) n -> p k n", p=128)
        for b in range(4):
            add(win[:, :, b * 512:(b + 1) * 512], 16, 512, W["win"][b], ("win", b))
        for g in range(4):
            add(I["pool_lin"][0][g].rearrange("(c p) d -> p c d", p=128), 2, 256, W["plin"][g], ("plin", g))
            add(I["fourier_lin"][0][g].rearrange("(c p) d -> p c d", p=128), 2, 256, W["flin"][g], ("flin", g))
        wo = I["pf_w_out"][0].rearrange("(k p) n -> p k n", p=128)
        for b in range(4):
            add(wo[:, :, b * 512:(b + 1) * 512], 16, 512, W["wout0"][b], ("wout0", b))
        for l in range(2):
            wg = I["ffn_w_gate"][l].rearrange("(k p) n -> p k n", p=128)
            wu = I["ffn_w_up"][l].rearrange("(k p) n -> p k n", p=128)
            wd = I["ffn_w_down"][l].rearrange("(f p) d -> p f d", p=128)
            for b in range(22):
                add(wg[:, :, b * 256:(b + 1) * 256], 16, 256, W["wg"][l, b], ("wg", l, b))
                add(wu[:, :, b * 256:(b + 1) * 256], 16, 256, W["wu"][l, b], ("wu", l, b))
            for b in range(16):
                add(wd[:, :, b * 128:(b + 1) * 128], 44, 128, W["wd"][l, b], ("wd", l, b))
            if l == 0:
                wa = I["da_w_in"][0].rearrange("(k p) n -> p k n", p=128)
                for h in range(8):
                    for part in range(3):
                        c0 = part * 2048 + h * 256
                        add(wa[:, :, c0:c0 + 256], 16, 256, W["wa"][h, part], ("wa", h, part))
                wo = I["da_w_out"][0].rearrange("(k p) n -> p k n", p=128)
                for b in range(4):
                    add(wo[:, :, b * 512:(b + 1) * 512], 16, 512, W["wout1"][b], ("wout1", b))
        with ExitStack() as st:
            stg = [(self.sb(st, "cst%d" % i, [128, 8192], F32), Buf("cst%d" % i)) for i in range(3)]
            bfs = [(self.sb(st, "cbf%d" % i, [128, 8192], BF16), Buf("cbf%d" % i)) for i in range(3)]

            def mkload(src, kc, cw):
                def f(t):
                    t3 = t[:, 0:kc * cw].rearrange("p (k c) -> p k c", k=kc)
                    step = max(1, 1536 // 128) if cw < 512 else kc
                    step = min(step, kc)
                    return [(t3[:, k0:min(kc, k0 + step), :], src[:, k0:min(kc, k0 + step), :], [])
                            for k0 in range(0, kc, step)]
                return f
            engs = os.environ.get("KENG", "dve,act").split(",")
            jobs = jobs[:int(os.environ.get("KJOBS", "100000"))]
            strm = Stream(S, stg, [mkload(j[0], j[1], j[2]) for j in jobs])
            for i, (src, kc, cw, dst, key) in enumerate(jobs):
                t, tb = strm.get(i)
                o, ob = bfs[i % 3]
                n = kc * cw
                e = engs[i % len(engs)]
                if e == "act":
                    S.op("act", lambda en: en.copy(out=o[:, 0:n], in_=t[:, 0:n]), reads=[tb], writes=[ob])
                else:
                    S.op(e, lambda en: en.tensor_copy(out=o[:, 0:n], in_=t[:, 0:n]), reads=[tb], writes=[ob])
                S.dma(dst, o[:, 0:n], reads=[ob], writes=[self.Wb[key]])
            S.barrier()

    def wload(self, key, dram_ap):
        return lambda t: [(t[:, 0:dram_ap.shape[-1]], dram_ap, [self.Wb[key]])]

    def phase_xin(self):
        nc, S, I = self.nc, self.S, self.I
        with ExitStack() as st:
            xin = [(self.sb(st, "xin%d" % i, [128, D], F32), Buf("xin%d" % i)) for i in range(3)]
            xts = [(self.sb(st, "xts%d" % i, [128, 16, 512], F32), Buf("xts%d" % i)) for i in range(2)]
            hts = [(self.sb(st, "hts%d" % i, [128, 16, 512], BF16), Buf("hts%d" % i)) for i in range(2)]
            A = self.sb(st, "xinA", [128, 3, 16], F32)
            vb = Buf("xinA")
            for s in range(3):
                S.op("dve", lambda e: e.tensor_scalar(out=A[:, s, :], in0=self.modv(0, 1, s), scalar1=1.0, scalar2=None,
                                                      op0=ALU.add), reads=[self.gb], writes=[vb])
            loads = []
            for tt, (s, j, ntok) in enumerate(self.tiles):
                for sub in range(ntok // 128):
                    if j < 4:
                        src = I["x"][s, j * 512 + sub * 128: j * 512 + (sub + 1) * 128, :]
                    else:
                        src = I["ctx"][s, sub * 128:(sub + 1) * 128, :]
                    loads.append((lambda src: (lambda t: [(t[:], src, [])]))(src))
            strm = Stream(S, xin, loads)
            idx = 0
            for tt, (s, j, ntok) in enumerate(self.tiles):
                src_s = s if j < 4 else 2
                xt, xb = xts[tt % 2]
                ht, hb = hts[tt % 2]
                for sub in range(ntok // 128):
                    t, tb = strm.get(idx)
                    idx += 1
                    for b in range(4):
                        pt, pb = self.bank()
                        S.sync("pe", reads=[tb, self.gb], writes=[pb])
                        for q in range(4):
                            c = 4 * b + q
                            ins = nc.tensor.transpose(out=pt[:, q * 128:(q + 1) * 128], in_=t[:, c * 128:(c + 1) * 128],
                                                      identity=self.ident[:])
                        S.done("pe", ins, reads=[tb, self.gb], writes=[pb])
                        S.op("act", lambda e: e.mul(out=xt[:, 4 * b:4 * b + 4, sub * 128:(sub + 1) * 128],
                                                    in_=pt[:, 0:512].rearrange("p (q n) -> p q n", q=4), mul=ALPHA),
                             reads=[pb], writes=[xb])
                        for q in range(4):
                            c = 4 * b + q
                            S.op("dve", lambda e: e.tensor_scalar(
                                out=ht[:, c, sub * 128:(sub + 1) * 128], in0=pt[:, q * 128:(q + 1) * 128],
                                scalar1=A[:, src_s, c:c + 1], scalar2=self.modT[0][:, 0 * 16 + c:0 * 16 + c + 1, src_s],
                                op0=ALU.mult, op1=ALU.add), reads=[pb, vb, self.gb], writes=[hb])
                S.dma(self.tile_ap(self.XT[0], tt, ntok), xt[:, :, 0:ntok], reads=[xb], writes=[self.XTb[0][tt]])
                S.dma(self.tile_ap(self.HT[0], tt, ntok), ht[:, :, 0:ntok], reads=[hb], writes=[self.HTb[0][tt]])
            S.barrier()
        self.dump("xt0", self.XT[0], sum(self.XTb[0:1], []), F32)
        self.dump("ht0", self.HT[0], sum(self.HTb[0:1], []), BF16)

    def ep_vecs(self, st, l, stage, final):
        S = self.S
        V = self.sb(st, self.name("epv"), [128, 3, 5, 16], F32)
        vb = Buf("epv")
        lg = self.vecT[:, 4 * l + 2 * (stage - 1), :]
        lb = self.vecT[:, 4 * l + 2 * (stage - 1) + 1, :]
        for s in range(3):
            gv = self.modv(l, 2 if stage == 1 else 5, s)
            S.op("dve", lambda e: e.tensor_copy(out=V[:, s, 0, :], in_=gv), reads=[self.gb], writes=[vb])
            if not final:
                if stage == 1:
                    sc, sh = self.modv(l, 4, s), self.modv(l, 3, s)
                else:
                    sc, sh = self.modv(l + 1, 1, s), self.modv(l + 1, 0, s)
                S.op("dve", lambda e: e.scalar_tensor_tensor(out=V[:, s, 1, :], in0=sc, scalar=1.0, in1=lg,
                                                             op0=ALU.add, op1=ALU.mult), reads=[self.gb], writes=[vb])
                S.op("dve", lambda e: e.scalar_tensor_tensor(out=V[:, s, 2, :], in0=sc, scalar=1.0, in1=lb,
                                                             op0=ALU.add, op1=ALU.mult), reads=[self.gb], writes=[vb])
                S.op("dve", lambda e: e.tensor_tensor(out=V[:, s, 2, :], in0=V[:, s, 2, :], in1=sh, op=ALU.add),
                     reads=[self.gb], writes=[vb])
            a = 1.0 if final else ALPHA
            S.op("dve", lambda e: e.tensor_scalar(out=V[:, s, 3, :], in0=lg, scalar1=a, scalar2=None, op0=ALU.mult),
                 reads=[self.gb], writes=[vb])
            S.op("dve", lambda e: e.tensor_scalar(out=V[:, s, 4, :], in0=lb, scalar1=a, scalar2=None, op0=ALU.mult),
                 reads=[self.gb], writes=[vb])
        return V, vb

    class Epi:
        def __init__(self, mk, st, l, stage, final, nbuf_y, nbuf_h, xt_in, xt_out, ht_out):
            self.mk = mk
            self.final = final
            self.V, self.vb = mk.ep_vecs(st, l, stage, final)
            self.ys = [(mk.sb(st, mk.name("epy"), [128, 16, 512], F32), [Buf("epy%d" % c) for c in range(16)])
                       for i in range(nbuf_y)]
            if not final:
                self.hs = [(mk.sb(st, mk.name("eph"), [128, 16, 512], BF16), [Buf("eph%d" % c) for c in range(16)])
                           for i in range(nbuf_h)]
            else:
                self.os = [(mk.sb(st, mk.name("epo"), [128, D], F32), Buf("epo")) for i in range(2)]
                self.on = 0
            self.sq = [(mk.sb(st, mk.name("epsq"), [128, 512], F32), Buf("epsq")) for i in range(3)]
            self.mean = mk.sb(st, mk.name("epm"), [128, 512], F32)
            self.rstd = mk.sb(st, mk.name("epr"), [128, 512], F32)
            self.mr = mk.sb(st, mk.name("epmr"), [128, 512], F32)
            self.stb = Buf("epstat")
            self.xt_in, self.xt_out, self.ht_out = xt_in, xt_out, ht_out
            self.n = 0
            self.sqn = 0
            self.pending = None

        def begin(self, tt, ntok, src_s):
            mk, S = self.mk, self.mk.S
            self.tt, self.ntok, self.src_s = tt, ntok, src_s
            self.y, self.yb = self.ys[self.n % len(self.ys)]
            if not self.final:
                self.h, self.hb = self.hs[self.n % len(self.hs)]
            self.n += 1
            S.dma(self.y[:, :, 0:ntok], mk.tile_ap(mk.XT[self.xt_in], tt, ntok), reads=[mk.XTb[self.xt_in][tt]],
                  writes=self.yb)

        def _stats(self, c, sq, sqb, last):
            mk, S, nc = self.mk, self.mk.S, self.mk.nc
            nt = self.ntok
            S.sync("pe", reads=[self.yb[c], sqb, mk.gb], writes=[mk.psb[6], mk.psb[7]])
            nc.tensor.matmul(mk.ps[6][:, 0:nt], lhsT=mk.ones_f[:], rhs=self.y[:, c, 0:nt],
                             start=(c == 0), stop=last)
            ins = nc.tensor.matmul(mk.ps[7][:, 0:nt], lhsT=mk.ones_f[:], rhs=sq[:, 0:nt],
                                   start=(c == 0), stop=last)
            S.done("pe", ins, reads=[self.yb[c], sqb, mk.gb], writes=[mk.psb[6], mk.psb[7]])

        def flush(self):
            if self.pending is not None:
                self._stats(*self.pending)
                self.pending = None

        def chunk(self, c, pt, pb):
            mk, S = self.mk, self.mk.S
            nt = self.ntok
            self.flush()
            y, yb = self.y, self.yb
            G = self.V[:, self.src_s, 0, c:c + 1]
            S.op("dve", lambda e: e.scalar_tensor_tensor(out=y[:, c, 0:nt], in0=pt[:, 0:nt], scalar=G, in1=y[:, c, 0:nt],
                                                         op0=ALU.mult, op1=ALU.add),
                 reads=[pb, self.vb, yb[c]], writes=[yb[c]])
            sq, sqb = self.sq[self.sqn % 3]
            self.sqn += 1
            S.op("act", lambda e: e.activation(out=sq[:, 0:nt], in_=y[:, c, 0:nt], func=AF.Square), reads=[yb[c]],
                 writes=[sqb])
            self.pending = (c, sq, sqb, c == 15)

        def finish(self):
            mk, S, nc = self.mk, self.mk.S, self.mk.nc
            nt, tt, s = self.ntok, self.tt, self.src_s
            self.flush()
            y, yb = self.y, self.yb
            mean, rstd, mr = self.mean, self.rstd, self.mr
            S.op("act", lambda e: e.copy(out=mean[:, 0:nt], in_=mk.ps[6][:, 0:nt]), reads=[mk.psb[6]], writes=[self.stb])
            S.op("dve", lambda e: e.tensor_tensor(out=rstd[:, 0:nt], in0=mean[:, 0:nt], in1=mean[:, 0:nt], op=ALU.mult),
                 reads=[self.stb], writes=[self.stb])
            S.op("dve", lambda e: e.tensor_tensor(out=rstd[:, 0:nt], in0=mk.ps[7][:, 0:nt], in1=rstd[:, 0:nt],
                                                  op=ALU.subtract), reads=[mk.psb[7], self.stb], writes=[self.stb])
            S.op("dve", lambda e: e.tensor_scalar(out=rstd[:, 0:nt], in0=rstd[:, 0:nt], scalar1=LN_EPS, scalar2=None,
                                                  op0=ALU.add), reads=[self.stb], writes=[self.stb])
            S.op("act", lambda e: e.activation(out=rstd[:, 0:nt], in_=rstd[:, 0:nt], func=AF.Sqrt), reads=[self.stb],
                 writes=[self.stb])
            S.op("dve", lambda e: e.reciprocal(out=rstd[:, 0:nt], in_=rstd[:, 0:nt]), reads=[self.stb], writes=[self.stb])
            S.op("dve", lambda e: e.tensor_tensor(out=mr[:, 0:nt], in0=mean[:, 0:nt], in1=rstd[:, 0:nt], op=ALU.mult),
                 reads=[self.stb], writes=[self.stb])
            V = self.V
            for c in range(16):
                S.op("pool", lambda e: e.tensor_tensor(out=y[:, c, 0:nt], in0=y[:, c, 0:nt], in1=rstd[:, 0:nt], op=ALU.mult),
                     reads=[self.stb, yb[c]], writes=[yb[c]])
                S.op("pool", lambda e: e.tensor_tensor(out=y[:, c, 0:nt], in0=y[:, c, 0:nt], in1=mr[:, 0:nt],
                                                       op=ALU.subtract), reads=[self.stb, yb[c]], writes=[yb[c]])
                if not self.final:
                    h, hb = self.h, self.hb
                    S.op("act", lambda e: e.activation(out=h[:, c, 0:nt], in_=y[:, c, 0:nt], func=AF.Identity,
                                                       scale=V[:, s, 1, c:c + 1], bias=V[:, s, 2, c:c + 1]),
                         reads=[yb[c], self.vb], writes=[hb[c]])
                S.op("dve", lambda e: e.tensor_scalar(out=y[:, c, 0:nt], in0=y[:, c, 0:nt], scalar1=V[:, s, 3, c:c + 1],
                                                      scalar2=V[:, s, 4, c:c + 1], op0=ALU.mult, op1=ALU.add),
                     reads=[yb[c], self.vb], writes=[yb[c]])
            if not self.final:
                S.dma(mk.tile_ap(mk.XT[self.xt_out], tt, nt), y[:, :, 0:nt], reads=yb, writes=[mk.XTb[self.xt_out][tt]])
                if self.ht_out is not None:
                    S.dma(mk.tile_ap(mk.HT[self.ht_out], tt, nt), self.h[:, :, 0:nt], reads=self.hb,
                          writes=[mk.HTb[self.ht_out][tt]])
            else:
                sidx, j, _ = mk.tiles[tt]
                for sub in range(nt // 128):
                    o, ob = self.os[self.on % 2]
                    self.on += 1
                    for b in range(4):
                        pt, pb = mk.bank()
                        S.sync("pe", reads=yb[4 * b:4 * b + 4] + [mk.gb], writes=[pb])
                        for q in range(4):
                            c = 4 * b + q
                            ins = nc.tensor.transpose(out=pt[:, q * 128:(q + 1) * 128],
                                                      in_=y[:, c, sub * 128:(sub + 1) * 128], identity=mk.ident[:])
                        S.done("pe", ins, reads=yb[4 * b:4 * b + 4] + [mk.gb], writes=[pb])
                        if b % 2 == 0:
                            S.op("act", lambda e: e.copy(out=o[:, b * 512:(b + 1) * 512], in_=pt[:, 0:512]), reads=[pb],
                                 writes=[ob])
                        else:
                            S.op("dve", lambda e: e.tensor_copy(out=o[:, b * 512:(b + 1) * 512], in_=pt[:, 0:512]),
                                 reads=[pb], writes=[ob])
                    t0 = j * 512 + sub * 128
                    S.dma(mk.out[sidx, t0:t0 + 128, :], o[:], reads=[ob], writes=[mk.outb])

    def phase_proj_ep(self, wkey, wdram, yt_idx, l, xt_in, xt_out, ht_out, tile_ids):
        nc, S = self.nc, self.S
        with ExitStack() as st:
            wt = self.sb(st, self.name("pw"), [128, 16, 4, 512], BF16)
            wb = Buf("pw")
            for b in range(4):
                S.dma(wt[:, :, b, :], wdram[b].rearrange("p (k c) -> p k c", k=16), reads=[self.Wb[(wkey, b)]], writes=[wb])
            ins_t = [(self.sb(st, self.name("pin"), [128, 16, 512], BF16), Buf("pin")) for i in range(2)]
            ep = MK.Epi(self, st, l, 1, False, 2, 1, xt_in, xt_out, ht_out)
            loads = []
            for tt in tile_ids:
                s, j, ntok = self.tiles[tt]
                loads.append((lambda tt, ntok: (lambda t: [(t[:, :, 0:ntok], self.tile_ap(self.YT[yt_idx], tt, ntok),
                                                            [self.YTb[yt_idx][tt]])]))(tt, ntok))
            strm = Stream(S, ins_t, loads)
            self.nbanks = 6
            for i, tt in enumerate(tile_ids):
                s, j, ntok = self.tiles[tt]
                src_s = s if j < 4 else 2
                ep.begin(tt, ntok, src_s)
                t, tb = strm.get(i)
                for c in range(16):
                    pt, pb = self.bank()
                    S.sync("pe", reads=[tb, wb], writes=[pb])
                    for k in range(16):
                        ins = nc.tensor.matmul(pt[:, 0:ntok], lhsT=wt[:, k, c // 4, (c % 4) * 128:(c % 4 + 1) * 128],
                                               rhs=t[:, k, 0:ntok], start=(k == 0), stop=(k == 15))
                    S.done("pe", ins, reads=[tb, wb], writes=[pb])
                    ep.chunk(c, pt, pb)
                ep.finish()
            self.nbanks = 8
            S.barrier()

    def phase_ffn(self, l, ht_in, xt_in, xt_out, ht_out, tile_ids, final):
        nc, S, W = self.nc, self.S, self.W
        with ExitStack() as st:
            hin = [(self.sb(st, self.name("fh"), [128, 16, 512], BF16), Buf("fh")) for i in range(2)]
            actb = self.sb(st, self.name("fact"), [128, FC, 512], BF16)
            actbuf = [Buf("fact%d" % f) for f in range(FC)]
            wgs = [(self.sb(st, self.name("fwg"), [128, 16 * 256], BF16), Buf("fwg")) for i in range(2)]
            wus = [(self.sb(st, self.name("fwu"), [128, 16 * 256], BF16), Buf("fwu")) for i in range(2)]
            wds = [(self.sb(st, self.name("fwd"), [128, 44 * 128], BF16), Buf("fwd")) for i in range(2)]
            sgs = [(self.sb(st, self.name("fsg"), [128, 512], F32), Buf("fsg")) for i in range(3)]
            ep = MK.Epi(self, st, l, 2, final, 1, 1, xt_in, xt_out, ht_out)
            nt_ = len(tile_ids)
            hloads, gl, ul, dl = [], [], [], []
            for tt in tile_ids:
                s, j, ntok = self.tiles[tt]
                hloads.append((lambda tt, ntok: (lambda t: [(t[:, :, 0:ntok], self.tile_ap(self.HT[ht_in], tt, ntok),
                                                             [self.HTb[ht_in][tt]])]))(tt, ntok))
                for b in range(22):
                    gl.append(self.wload(("wg", l, b), W["wg"][l, b]))
                    ul.append(self.wload(("wu", l, b), W["wu"][l, b]))
                for b in range(16):
                    dl.append(self.wload(("wd", l, b), W["wd"][l, b]))
            hs = Stream(S, hin, hloads)
            gs = Stream(S, wgs, gl)
            us = Stream(S, wus, ul)
            ds = Stream(S, wds, dl)
            self.nbanks = 6
            sgn = 0
            for i, tt in enumerate(tile_ids):
                s, j, ntok = self.tiles[tt]
                src_s = s if j < 4 else 2
                h, hb = hs.get(i)
                for b in range(22):
                    wg, wgb = gs.get(i * 22 + b)
                    wu, wub = us.get(i * 22 + b)
                    wg3 = wg[:].rearrange("p (k c) -> p k c", k=16)
                    wu3 = wu[:].rearrange("p (k c) -> p k c", k=16)
                    for fc in range(2):
                        f = 2 * b + fc
                        pg, pgb = self.bank()
                        S.sync("pe", reads=[hb, wgb], writes=[pgb])
                        for k in range(16):
                            ins = nc.tensor.matmul(pg[:, 0:ntok], lhsT=wg3[:, k, fc * 128:(fc + 1) * 128],
                                                   rhs=h[:, k, 0:ntok], start=(k == 0), stop=(k == 15))
                        S.done("pe", ins, reads=[hb, wgb], writes=[pgb])
                        pu, pub = self.bank()
                        S.sync("pe", reads=[hb, wub], writes=[pub])
                        for k in range(16):
                            ins = nc.tensor.matmul(pu[:, 0:ntok], lhsT=wu3[:, k, fc * 128:(fc + 1) * 128],
                                                   rhs=h[:, k, 0:ntok], start=(k == 0), stop=(k == 15))
                        S.done("pe", ins, reads=[hb, wub], writes=[pub])
                        sg, sgb = sgs[sgn % 3]
                        sgn += 1
                        S.op("act", lambda e: e.activation(out=sg[:, 0:ntok], in_=pg[:, 0:ntok], func=AF.Silu),
                             reads=[pgb], writes=[sgb])
                        S.op("dve", lambda e: e.tensor_tensor(out=actb[:, f, 0:ntok], in0=pu[:, 0:ntok], in1=sg[:, 0:ntok],
                                                              op=ALU.mult), reads=[pub, sgb], writes=[actbuf[f]])
                ep.begin(tt, ntok, src_s)
                for db in range(16):
                    wd, wdb = ds.get(i * 16 + db)
                    wd3 = wd[:].rearrange("p (f c) -> p f c", f=FC)
                    pt, pb = self.bank()
                    S.sync("pe", reads=actbuf + [wdb], writes=[pb])
                    for f in range(FC):
                        ins = nc.tensor.matmul(pt[:, 0:ntok], lhsT=wd3[:, f, :], rhs=actb[:, f, 0:ntok],
                                               start=(f == 0), stop=(f == FC - 1))
                    S.done("pe", ins, reads=actbuf + [wdb], writes=[pb])
                    ep.chunk(db, pt, pb)
                ep.finish()
            self.nbanks = 8
            S.barrier()

    def phase_m1a(self):
        nc, S, W = self.nc, self.S, self.W
        with ExitStack() as st:
            wt = self.sb(st, "m1w", [128, 16, 2, 512], BF16)
            wb = Buf("m1w")
            for b in range(2):
                S.dma(wt[:, :, b, :], W["win"][2 + b].rearrange("p (k c) -> p k c", k=16), reads=[self.Wb[("win", 2 + b)]],
                      writes=[wb])
            dftc = self.sb(st, "m1dftc", [128, 2, 512], BF16)
            S.dma(dftc[:], self.C["dft_c"][:, :, :], writes=[wb])
            hin = [(self.sb(st, self.name("m1h"), [128, 16, 512], BF16), Buf("m1h")) for i in range(2)]
            ufs = [(self.sb(st, self.name("m1u"), [128, 8, 512], BF16), Buf("m1u")) for i in range(2)]
            pqs = [(self.sb(st, self.name("m1pq"), [128, 4, 512], BF16), Buf("m1pq")) for i in range(3)]
            loads = []
            for tt, (s, j, ntok) in enumerate(self.tiles):
                loads.append((lambda tt, ntok: (lambda t: [(t[:, :, 0:ntok], self.tile_ap(self.HT[0], tt, ntok),
                                                            [self.HTb[0][tt]])]))(tt, ntok))
            hs = Stream(S, hin, loads)
            pqn = 0
            for tt, (s, j, ntok) in enumerate(self.tiles):
                h, hb = hs.get(tt)
                uf, ub = ufs[tt % 2]
                for mc in range(8):
                    pt, pb = self.bank()
                    S.sync("pe", reads=[hb, wb], writes=[pb])
                    for k in range(16):
                        ins = nc.tensor.matmul(pt[:, 0:ntok], lhsT=wt[:, k, mc // 4, (mc % 4) * 128:(mc % 4 + 1) * 128],
                                               rhs=h[:, k, 0:ntok], start=(k == 0), stop=(k == 15))
                    S.done("pe", ins, reads=[hb, wb], writes=[pb])
                    S.op("act", lambda e: e.copy(out=uf[:, mc, 0:ntok], in_=pt[:, 0:ntok]), reads=[pb], writes=[ub])
                for sub in range(ntok // 128):
                    pq, pqb = pqs[pqn % 3]
                    pqn += 1
                    for g in range(4):
                        pt, pb = self.bank()
                        S.sync("pe", reads=[ub, wb], writes=[pb])
                        for jj in range(2):
                            ins = nc.tensor.matmul(pt[:, :], lhsT=uf[:, 2 * g + jj, sub * 128:(sub + 1) * 128],
                                                   rhs=dftc[:, jj, :], start=(jj == 0), stop=(jj == 1))
                        S.done("pe", ins, reads=[ub, wb], writes=[pb])
                        S.op("dve", lambda e: e.tensor_copy(out=pq[:, g, :], in_=pt[:, :]), reads=[pb], writes=[pqb])
                    ch = (j * 4 + sub) if j < 4 else (16 + sub)
                    S.dma(self.PQ[s, ch].rearrange("p (g c) -> p g c", g=4), pq[:], reads=[pqb], writes=[self.PQb[s][ch]])
            S.barrier()

    def phase_m1b(self):
        nc, S, W = self.nc, self.S, self.W
        for pg in range(2):
            with ExitStack() as st:
                wt = self.sb(st, self.name("pbw"), [128, 16, 512], BF16)
                wb = Buf("pbw")
                S.dma(wt[:], W["win"][pg].rearrange("p (k c) -> p k c", k=16), reads=[self.Wb[("win", pg)]], writes=[wb])
                plin = self.sb(st, self.name("pbl"), [128, 2, 2, 256], BF16)
                for gl in range(2):
                    S.dma(plin[:, gl, :, :], W["plin"][2 * pg + gl].rearrange("p (c d) -> p c d", c=2),
                          reads=[self.Wb[("plin", 2 * pg + gl)]], writes=[wb])
                PW = NTOK + 16
                up = self.sb(st, self.name("pbu"), [128, 4, PW], F32)
                upb = Buf("pbu")
                ta = self.sb(st, self.name("pba"), [128, 2, PW], F32)
                tb_ = self.sb(st, self.name("pbb"), [128, 2, PW], F32)
                tab = Buf("pbab")
                pooled = self.sb(st, self.name("pbp"), [128, 4, NTOK], BF16)
                pob = Buf("pbp")
                efac = self.sb(st, self.name("pbe"), [128, 2, 4, 16], F32)
                S.dma(efac[:, 0, :, :], self.C["efac_n"].rearrange("(o w) e -> o w e", o=1).broadcast_to([128, 4, 16]),
                      writes=[wb])
                S.dma(efac[:, 1, :, :], self.C["efac_x"].rearrange("(o w) e -> o w e", o=1).broadcast_to([128, 4, 16]),
                      writes=[wb])
                hin = [(self.sb(st, self.name("pbh"), [128, 16, 512], BF16), Buf("pbh")) for i in range(2)]
                yts = [(self.sb(st, self.name("pby"), [128, 4, 512], BF16), Buf("pby")) for i in range(2)]
                loads = []
                for tt, (s, j, ntok) in enumerate(self.tiles):
                    loads.append((lambda tt, ntok: (lambda t: [(t[:, :, 0:ntok], self.tile_ap(self.HT[0], tt, ntok),
                                                                [self.HTb[0][tt]])]))(tt, ntok))
                hs = Stream(S, hin, loads)
                ytn = 0
                for s in range(NS):
                    for kind in range(2):
                        N = NTOK if kind == 0 else NCTX
                        tids = [s * 5 + j for j in range(4)] if kind == 0 else [s * 5 + 4]
                        S.op("pool", lambda e: e.memset(up[:, :, 0:8], 0.0), reads=[upb], writes=[upb])
                        S.op("pool", lambda e: e.memset(up[:, :, 8 + N:16 + N], 0.0), reads=[upb], writes=[upb])
                        for tt in tids:
                            _, j, ntok = self.tiles[tt]
                            h, hb = hs.get(tt)
                            t0 = 8 + (j * 512 if kind == 0 else 0)
                            for mc in range(4):
                                pt, pb = self.bank()
                                S.sync("pe", reads=[hb, wb], writes=[pb])
                                for k in range(16):
                                    ins = nc.tensor.matmul(pt[:, 0:ntok], lhsT=wt[:, k, mc * 128:(mc + 1) * 128],
                                                           rhs=h[:, k, 0:ntok], start=(k == 0), stop=(k == 15))
                                S.done("pe", ins, reads=[hb, wb], writes=[pb])
                                S.op("act", lambda e: e.copy(out=up[:, mc, t0:t0 + ntok], in_=pt[:, 0:ntok]), reads=[pb],
                                     writes=[upb])
                        PWn = N + 16
                        for gl in range(2):
                            g = 2 * pg + gl
                            U = up[:, 2 * gl:2 * gl + 2, :]
                            eng = "dve" if gl == 0 else "pool"
                            S.op(eng, lambda e: e.tensor_tensor(out=ta[:, :, 1:PWn], in0=U[:, :, 0:PWn - 1], in1=U[:, :, 1:PWn],
                                                                op=ALU.add), reads=[upb, tab], writes=[tab])
                            R = ta
                            if g >= 1:
                                S.op(eng, lambda e: e.tensor_tensor(out=tb_[:, :, 2:PWn - 1], in0=ta[:, :, 1:PWn - 2],
                                                                    in1=ta[:, :, 3:PWn], op=ALU.add), reads=[tab], writes=[tab])
                                R = tb_
                            if g >= 2:
                                S.op(eng, lambda e: e.tensor_tensor(out=ta[:, :, 4:PWn - 3], in0=tb_[:, :, 2:PWn - 5],
                                                                    in1=tb_[:, :, 6:PWn - 1], op=ALU.add), reads=[tab],
                                     writes=[tab])
                                R = ta
                            if g >= 3:
                                S.op(eng, lambda e: e.tensor_tensor(out=tb_[:, :, 8:PWn - 8], in0=ta[:, :, 4:PWn - 12],
                                                                    in1=ta[:, :, 12:PWn - 4], op=ALU.add), reads=[tab],
                                     writes=[tab])
                                R = tb_
                            w = POOL_WINDOWS[g]
                            for side in range(2):
                                c0 = 8 if side == 0 else N
                                S.op(eng, lambda e: e.tensor_tensor(
                                    out=R[:, :, c0:c0 + 8], in0=R[:, :, c0:c0 + 8],
                                    in1=efac[:, kind, g, side * 8:(side + 1) * 8].unsqueeze(1).broadcast_to([128, 2, 8]),
                                    op=ALU.mult), reads=[tab, wb], writes=[tab])
                            S.op("dve", lambda e: e.scalar_tensor_tensor(out=pooled[:, 2 * gl:2 * gl + 2, 0:N], in0=R[:, :, 8:8 + N],
                                                                       scalar=1.0 / w, in1=U[:, :, 8:8 + N], op0=ALU.mult,
                                                                       op1=ALU.subtract), reads=[tab, upb, pob], writes=[pob])
                        for tt in tids:
                            _, j, ntok = self.tiles[tt]
                            k0 = (j * 512 if kind == 0 else 0)
                            yt, ytb = yts[ytn % 2]
                            ytn += 1
                            for gl in range(2):
                                g = 2 * pg + gl
                                for dc in range(2):
                                    pt, pb = self.bank()
                                    S.sync("pe", reads=[pob, wb], writes=[pb])
                                    for cc in range(2):
                                        ins = nc.tensor.matmul(pt[:, 0:ntok], lhsT=plin[:, gl, cc, dc * 128:(dc + 1) * 128],
                                                               rhs=pooled[:, 2 * gl + cc, k0:k0 + ntok], start=(cc == 0),
                                                               stop=(cc == 1))
                                    S.done("pe", ins, reads=[pob, wb], writes=[pb])
                                    ch = 2 * g + dc
                                    S.op("act", lambda e: e.mul(out=yt[:, 2 * gl + dc, 0:ntok], in_=pt[:, 0:ntok],
                                                                mul=self.vecT[:, 8, ch:ch + 1]), reads=[pb, self.gb],
                                         writes=[ytb])
                            S.dma(self.tile_ap(self.YT[0], tt, ntok)[:, 4 * pg:4 * pg + 4, :], yt[:, :, 0:ntok], reads=[ytb],
                                  writes=[self.YTb[0][tt]])
                S.barrier()

    def phase_m3(self):
        nc, S, W = self.nc, self.S, self.W
        with ExitStack() as st:
            flin = self.sb(st, "m3l", [128, 4, 2, 256], BF16)
            wb = Buf("m3l")
            for g in range(4):
                S.dma(flin[:, g, :, :], W["flin"][g].rearrange("p (c d) -> p c d", c=2), reads=[self.Wb[("flin", g)]],
                      writes=[wb])
            pq = self.sb(st, "m3pq", [128, 16, 4, 512], BF16)
            pqb = Buf("m3pq")
            tabs = [(self.sb(st, self.name("m3t"), [128, 16, 2, 512], BF16), Buf("m3t")) for i in range(2)]
            fts = [(self.sb(st, self.name("m3f"), [128, 8, 512], BF16), Buf("m3f")) for i in range(2)]
            yts = [(self.sb(st, self.name("m3y"), [128, 8, 512], BF16), Buf("m3y")) for i in range(2)]
            n = 0
            for s in range(NS):
                for kind in range(2):
                    nch = 16 if kind == 0 else 2
                    kt = 512 if kind == 0 else 256
                    tids = [s * 5 + j for j in range(4)] if kind == 0 else [s * 5 + 4]
                    for ch in range(nch):
                        chd = ch if kind == 0 else 16 + ch
                        S.dma(pq[:, ch, :, :], self.PQ[s, chd].rearrange("p (g c) -> p g c", g=4), reads=[self.PQb[s][chd]],
                              writes=[pqb])
                    for ki, tt in enumerate(tids):
                        tab, tbb = tabs[n % 2]
                        ft, ftb = fts[n % 2]
                        yt, ytb = yts[n % 2]
                        n += 1
                        if kind == 0:
                            S.dma(tab[:], self.C["dft_n"][ki].rearrange("p (c t k) -> p c t k", c=16, t=2), writes=[tbb])
                        else:
                            S.dma(tab[:, 0:2, :, 0:256], self.C["dft_x"][0].rearrange("p (c t k) -> p c t k", c=2, t=2),
                                  writes=[tbb])
                        for g in range(4):
                            for mh in range(2):
                                pt, pb = self.bank()
                                S.sync("pe", reads=[pqb, tbb], writes=[pb])
                                for c in range(nch):
                                    nc.tensor.matmul(pt[:, 0:kt], lhsT=pq[:, c, g, mh * 128:(mh + 1) * 128],
                                                     rhs=tab[:, c, 0, 0:kt], start=(c == 0), stop=False)
                                    ins = nc.tensor.matmul(pt[:, 0:kt], lhsT=pq[:, c, g, 256 + mh * 128:256 + (mh + 1) * 128],
                                                           rhs=tab[:, c, 1, 0:kt], start=False, stop=(c == nch - 1))
                                S.done("pe", ins, reads=[pqb, tbb], writes=[pb])
                                S.op("act" if mh == 0 else "dve",
                                     (lambda e: e.copy(out=ft[:, 2 * g + mh, 0:kt], in_=pt[:, 0:kt])) if mh == 0 else
                                     (lambda e: e.tensor_copy(out=ft[:, 2 * g + mh, 0:kt], in_=pt[:, 0:kt])),
                                     reads=[pb], writes=[ftb])
                        for g in range(4):
                            for dc in range(2):
                                pt, pb = self.bank()
                                S.sync("pe", reads=[ftb, wb], writes=[pb])
                                for mh in range(2):
                                    ins = nc.tensor.matmul(pt[:, 0:kt], lhsT=flin[:, g, mh, dc * 128:(dc + 1) * 128],
                                                           rhs=ft[:, 2 * g + mh, 0:kt], start=(mh == 0), stop=(mh == 1))
                                S.done("pe", ins, reads=[ftb, wb], writes=[pb])
                                S.op("act" if dc == 0 else "dve",
                                     (lambda e: e.copy(out=yt[:, 2 * g + dc, 0:kt], in_=pt[:, 0:kt])) if dc == 0 else
                                     (lambda e: e.tensor_copy(out=yt[:, 2 * g + dc, 0:kt], in_=pt[:, 0:kt])),
                                     reads=[pb], writes=[ytb])
                        S.dma(self.tile_ap(self.YT[0], tt, kt)[:, 8:16, :], yt[:, :, 0:kt], reads=[ytb],
                              writes=[self.YTb[0][tt]])
            S.barrier()
        self.dump("yt0", self.YT[0], self.YTb[0], BF16)

    def phase_attn(self):
        nc, S, W, I = self.nc, self.S, self.W, self.I
        with ExitStack() as st:
            ws = [(self.sb(st, self.name("aw"), [128, 16 * 256], BF16), Buf("aw")) for i in range(6)]
            hin = [(self.sb(st, self.name("ah"), [128, 16, 512], BF16), Buf("ah")) for i in range(2)]
            QT = self.sb(st, "aQT", [128, 2, NTOK], BF16)
            KT = self.sb(st, "aKT", [128, 2, NTOK + NCTX], BF16)
            VA = self.sb(st, "aVA", [128, 18, 264], BF16)
            qb_, kb_, vb_ = Buf("aQT"), Buf("aKT"), Buf("aVA")
            cosT = self.sb(st, "acos", [128, 16, 128], F32)
            sinT = self.sb(st, "asin", [128, 16, 128], F32)
            cb = Buf("aconst")
            S.dma(cosT[:], self.C["rope_cos"].rearrange("(j p) d -> p j d", p=128), writes=[cb])
            S.dma(sinT[:], self.C["rope_sin"].rearrange("(j p) d -> p j d", p=128), writes=[cb])
            S.op("dve", lambda e: e.memset(VA[:, :, 256:264], 0.0), writes=[vb_])
            S.op("dve", lambda e: e.memset(VA[:, :, 256:257], 1.0), writes=[vb_])
            lv = self.sb(st, "alv", [128, 4, 128], F32)
            for i, nm in enumerate(("da_lam_q1", "da_lam_k1", "da_lam_q2", "da_lam_k2")):
                S.dma(lv[:, i, :], I[nm][0:1, :].broadcast_to([128, 128]), writes=[cb])
            lsc = self.sb(st, "alsc", [128, 8], F32)
            S.op("dve", lambda e: e.tensor_tensor(out=lv[:, 0, :], in0=lv[:, 0, :], in1=lv[:, 1, :], op=ALU.mult),
                 reads=[cb], writes=[cb])
            S.op("dve", lambda e: e.tensor_tensor(out=lv[:, 2, :], in0=lv[:, 2, :], in1=lv[:, 3, :], op=ALU.mult),
                 reads=[cb], writes=[cb])
            S.op("dve", lambda e: e.reduce_sum(out=lsc[:, 0:1], in_=lv[:, 0, :], axis=AX.X), reads=[cb], writes=[cb])
            S.op("dve", lambda e: e.reduce_sum(out=lsc[:, 1:2], in_=lv[:, 2, :], axis=AX.X), reads=[cb], writes=[cb])
            S.op("act", lambda e: e.activation(out=lsc[:, 2:4], in_=lsc[:, 0:2], func=AF.Exp), reads=[cb], writes=[cb])
            S.op("dve", lambda e: e.scalar_tensor_tensor(out=lsc[:, 4:5], in0=lsc[:, 3:4], scalar=-LAMBDA_INIT1,
                                                         in1=lsc[:, 2:3], op0=ALU.add, op1=ALU.subtract),
                 reads=[cb], writes=[cb])
            Gs = self.sb(st, "aGs", [128, 256], F32)
            S.dma(Gs[:], I["da_subln_g"][0:1, :].broadcast_to([128, 256]), writes=[cb])
            S.op("dve", lambda e: e.tensor_scalar(out=Gs[:], in0=Gs[:], scalar1=(1.0 - LAMBDA_INIT1), scalar2=None,
                                                  op0=ALU.mult), reads=[cb], writes=[cb])
            ssq = self.sb(st, "assq", [128, 18, 4], F32)
            ssb = Buf("assq")
            mm = self.sb(st, "amm", [128, 8], F32)
            negM = self.sb(st, "anegM", [128, 1], F32)
            mb = Buf("amm")
            xs = [(self.sb(st, self.name("axs"), [128, 768], F32), Buf("axs")) for i in range(2)]
            ra = [(self.sb(st, self.name("ara"), [128, 512], F32), Buf("ara")) for i in range(2)]
            rb = [(self.sb(st, self.name("arb"), [128, 512], F32), Buf("arb")) for i in range(2)]
            qk = [(self.sb(st, self.name("aqk"), [128, 512], BF16), Buf("aqk")) for i in range(2)]
            ET = [self.sb(st, "aET%d" % m, [128, 18, 512], BF16) for m in range(2)]
            etb = [[Buf("aET%d_%d" % (m, kc)) for kc in range(18)] for m in range(2)]
            t0s = [(self.sb(st, self.name("at0"), [128, 256], F32), Buf("at0")) for i in range(2)]
            os_ = [(self.sb(st, self.name("aos"), [128, 256], F32), Buf("aos")) for i in range(2)]
            sc_ = [(self.sb(st, self.name("asc"), [128, 8], F32), Buf("asc")) for i in range(2)]
            ab = [(self.sb(st, self.name("aab"), [128, 256], BF16), Buf("aab")) for i in range(2)]
            att = [(self.sb(st, self.name("aatt"), [128, 2, 512], BF16), Buf("aatt")) for i in range(2)]
            wloads, hloads = [], []
            for s in range(NS):
                for h in range(8):
                    for part in range(3):
                        wloads.append(self.wload(("wa", h, part), W["wa"][h, part]))
                    for j in range(5):
                        tt = s * 5 + j
                        ntok = self.tiles[tt][2]
                        hloads.append((lambda tt, ntok: (lambda t: [(t[:, :, 0:ntok], self.tile_ap(self.HT[2], tt, ntok),
                                                                    [self.HTb[2][tt]])]))(tt, ntok))
            wstr = Stream(S, ws, wloads, ahead=3)
            hstr = Stream(S, hin, hloads)
            n_x = 0
            n_att = 0
            n_pv = 0
            for s in range(NS):
                for h in range(8):
                    ih = s * 8 + h
                    wq, wqb = wstr.get(ih * 3 + 0)
                    wk, wkb = wstr.get(ih * 3 + 1)
                    wv, wvb = wstr.get(ih * 3 + 2)
                    w3 = [w[:].rearrange("p (k c) -> p k c", k=16) for w in (wq, wk, wv)]
                    S.op("pool", lambda e: e.memset(ssq[:], 0.0), reads=[ssb], writes=[ssb])
                    for j in range(5):
                        tt = s * 5 + j
                        ntok = self.tiles[tt][2]
                        hT, hb = hstr.get(ih * 5 + j)
                        lat = j < 4
                        for sub in range(ntok // 128):
                            ch = (j * 4 + sub) if lat else (16 + sub)
                            x_, xb_ = xs[n_x % 2]
                            r_a, rab = ra[n_x % 2]
                            r_b, rbb = rb[n_x % 2]
                            q_k, qkb = qk[n_x % 2]
                            n_x += 1
                            parts = ([0, 1, 2] if lat else [1, 2])
                            for part in parts:
                                pt, pb = self.bank()
                                wbuf = (wqb, wkb, wvb)[part]
                                S.sync("pe", reads=[hb, wbuf], writes=[pb])
                                for k in range(16):
                                    ins = nc.tensor.matmul(pt[:, 0:256], lhsT=hT[:, k, sub * 128:(sub + 1) * 128],
                                                           rhs=w3[part][:, k, :], start=(k == 0), stop=(k == 15))
                                S.done("pe", ins, reads=[hb, wbuf], writes=[pb])
                                if part == 2:
                                    S.op("act", lambda e: e.copy(out=VA[:, ch, 0:256], in_=pt[:, 0:256]), reads=[pb],
                                         writes=[vb_])
                                else:
                                    S.op("act", lambda e: e.copy(out=x_[:, part * 256:(part + 1) * 256], in_=pt[:, 0:256]),
                                         reads=[pb], writes=[xb_])
                            c0 = 0 if lat else 256
                            nb = 4 if lat else 2
                            W_ = nb * 128
                            xv = x_[:, c0:c0 + W_]
                            if lat:
                                x5 = xv.rearrange("p (b r h d) -> p b r h d", b=nb, r=2, h=2)
                                a5 = r_a[:, 0:W_].rearrange("p (b r h d) -> p b r h d", b=nb, r=2, h=2)
                                b5 = r_b[:, 0:W_].rearrange("p (b r h d) -> p b r h d", b=nb, r=2, h=2)
                                cos4 = cosT[:, ch, :].unsqueeze(1).broadcast_to([128, nb, 128])
                                sin5 = sinT[:, ch, :].rearrange("p (r h d) -> p r h d", r=2, h=2)
                                S.op("dve", lambda e: e.tensor_tensor(out=r_a[:, 0:W_].rearrange("p (b d) -> p b d", b=nb),
                                                                      in0=xv.rearrange("p (b d) -> p b d", b=nb), in1=cos4,
                                                                      op=ALU.mult), reads=[xb_, cb], writes=[rab])
                                for hh in range(2):
                                    S.op("pool", lambda e: e.tensor_tensor(
                                        out=b5[:, :, :, hh, :], in0=x5[:, :, :, 1 - hh, :],
                                        in1=sin5[:, :, hh, :].unsqueeze(1).broadcast_to([128, nb, 2, 32]), op=ALU.mult),
                                        reads=[xb_, cb], writes=[rbb])
                                S.op("dve", lambda e: e.tensor_tensor(out=r_a[:, 0:W_], in0=r_a[:, 0:W_], in1=r_b[:, 0:W_],
                                                                      op=ALU.add), reads=[rab, rbb], writes=[rab])
                                src, srcb = r_a[:, 0:W_], rab
                            else:
                                src, srcb = xv, xb_
                            S.op("act", lambda e: e.copy(out=q_k[:, 0:W_], in_=src), reads=[srcb], writes=[qkb])
                            sqd = r_b[:, 0:W_]
                            S.op("dve", lambda e: e.tensor_tensor(out=sqd, in0=src, in1=src, op=ALU.mult), reads=[srcb, rbb],
                                 writes=[rbb])
                            S.op("dve", lambda e: e.reduce_sum(out=ssq[:, ch, (4 - nb):4],
                                                               in_=sqd.rearrange("p (b d) -> p b d", b=nb), axis=AX.X),
                                 reads=[rbb, ssb], writes=[ssb])
                            pt, pb = self.bank()
                            ptb = pt[:].bitcast(BF16)
                            S.sync("pe", reads=[qkb, self.gb], writes=[pb])
                            for bi in range(nb):
                                ins = nc.tensor.transpose(out=ptb[:, bi * 128:(bi + 1) * 128], in_=q_k[:, bi * 128:(bi + 1) * 128],
                                                          identity=self.identb[:])
                            S.done("pe", ins, reads=[qkb, self.gb], writes=[pb])
                            tk0 = ch * 128
                            if lat:
                                S.op("dve", lambda e: e.tensor_copy(out=QT[:, :, tk0:tk0 + 128],
                                                                    in_=ptb[:, 0:256].rearrange("p (m t) -> p m t", m=2)),
                                     reads=[pb, qb_], writes=[qb_])
                                S.op("act", lambda e: e.copy(out=KT[:, :, tk0:tk0 + 128],
                                                             in_=ptb[:, 256:512].rearrange("p (m t) -> p m t", m=2)),
                                     reads=[pb, kb_], writes=[kb_])
                            else:
                                S.op("act", lambda e: e.copy(out=KT[:, :, tk0:tk0 + 128],
                                                             in_=ptb[:, 0:256].rearrange("p (m t) -> p m t", m=2)),
                                     reads=[pb, kb_], writes=[kb_])
                    S.op("dve", lambda e: e.reduce_max(out=mm[:, 0:1], in_=ssq[:, :, 0:2], axis=AX.XY), reads=[ssb, mb],
                         writes=[mb])
                    S.op("dve", lambda e: e.reduce_max(out=mm[:, 1:2], in_=ssq[:, :, 2:4], axis=AX.XY), reads=[ssb, mb],
                         writes=[mb])
                    pt, pb = self.bank()
                    S.sync("pe", reads=[mb, self.gb], writes=[pb])
                    ins = nc.tensor.transpose(out=pt[0:2, 0:128], in_=mm[:, 0:2], identity=self.ident[:])
                    S.done("pe", ins, reads=[mb, self.gb], writes=[pb])
                    S.op("dve", lambda e: e.reduce_max(out=mm[0:2, 2:3], in_=pt[0:2, 0:128], axis=AX.X), reads=[pb, mb],
                         writes=[mb])
                    S.op("dve", lambda e: e.tensor_scalar(out=mm[0:2, 4:6], in0=self.ident[0:2, 0:2], scalar1=mm[0:2, 2:3],
                                                          scalar2=None, op0=ALU.mult), reads=[mb, self.gb], writes=[mb])
                    pt, pb = self.bank()
                    S.sync("pe", reads=[mb, self.gb], writes=[pb])
                    ins = nc.tensor.matmul(pt[:, 0:2], lhsT=self.ones_f[0:2, :], rhs=mm[0:2, 4:6], start=True, stop=True)
                    S.done("pe", ins, reads=[mb, self.gb], writes=[pb])
                    S.op("dve", lambda e: e.tensor_copy(out=mm[:, 6:8], in_=pt[:, 0:2]), reads=[pb, mb], writes=[mb])
                    S.op("dve", lambda e: e.tensor_tensor(out=mm[:, 3:4], in0=mm[:, 6:7], in1=mm[:, 7:8], op=ALU.mult),
                         reads=[mb], writes=[mb])
                    S.op("act", lambda e: e.activation(out=mm[:, 3:4], in_=mm[:, 3:4], func=AF.Sqrt), reads=[mb], writes=[mb])
                    S.op("dve", lambda e: e.tensor_scalar(out=negM[:], in0=mm[:, 3:4], scalar1=-SM_SCALE * float(D), scalar2=None,
                                                          op0=ALU.mult), reads=[mb], writes=[mb])
                    for qt in range(4):
                        tt = s * 5 + qt
                        for m in range(2):
                            for kc in range(18):
                                pt, pb = self.bank()
                                S.sync("pe", reads=[qb_, kb_], writes=[pb])
                                ins = nc.tensor.matmul(pt[:, :], lhsT=KT[:, m, kc * 128:(kc + 1) * 128],
                                                       rhs=QT[:, m, qt * 512:(qt + 1) * 512], start=True, stop=True)
                                S.done("pe", ins, reads=[qb_, kb_], writes=[pb])
                                S.op("act", lambda e: e.activation(out=ET[m][:, kc, :], in_=pt[:, :], func=AF.Exp,
                                                                   bias=negM[:, 0:1], scale=SM_SCALE), reads=[pb, mb],
                                     writes=[etb[m][kc]])
                        at_, atb = att[n_att % 2]
                        n_att += 1
                        for qs in range(4):
                            pv = []
                            for m in range(2):
                                pt, pb = self.bank()
                                S.sync("pe", reads=etb[m] + [vb_], writes=[pb])
                                for kc in range(18):
                                    ins = nc.tensor.matmul(pt[:, 0:258], lhsT=ET[m][:, kc, qs * 128:(qs + 1) * 128],
                                                           rhs=VA[:, kc, 0:258], start=(kc == 0), stop=(kc == 17))
                                S.done("pe", ins, reads=etb[m] + [vb_], writes=[pb])
                                pv.append((pt, pb))
                            t0, t0b = t0s[n_pv % 2]
                            o_, ob_ = os_[n_pv % 2]
                            sc, scb = sc_[n_pv % 2]
                            a_, abb = ab[n_pv % 2]
                            n_pv += 1
                            (p0, p0b), (p1, p1b) = pv
                            S.op("dve", lambda e: e.reciprocal(out=sc[:, 0:1], in_=p0[:, 256:257]), reads=[p0b, scb],
                                 writes=[scb])
                            S.op("dve", lambda e: e.reciprocal(out=sc[:, 1:2], in_=p1[:, 256:257]), reads=[p1b, scb],
                                 writes=[scb])
                            S.op("dve", lambda e: e.tensor_tensor(out=sc[:, 2:3], in0=sc[:, 1:2], in1=lsc[:, 4:5], op=ALU.mult),
                                 reads=[scb, cb], writes=[scb])
                            S.op("act", lambda e: e.mul(out=t0[:], in_=p0[:, 0:256], mul=sc[:, 0:1]), reads=[p0b, scb],
                                 writes=[t0b])
                            S.op("dve", lambda e: e.scalar_tensor_tensor(out=o_[:], in0=p1[:, 0:256], scalar=sc[:, 2:3],
                                                                         in1=t0[:], op0=ALU.mult, op1=ALU.add),
                                 reads=[p1b, scb, t0b], writes=[ob_])
                            S.op("dve", lambda e: e.tensor_tensor(out=t0[:], in0=o_[:], in1=o_[:], op=ALU.mult),
                                 reads=[ob_, t0b], writes=[t0b])
                            S.op("dve", lambda e: e.reduce_sum(out=sc[:, 3:4], in_=t0[:], axis=AX.X), reads=[t0b, scb],
                                 writes=[scb])
                            S.op("dve", lambda e: e.tensor_scalar(out=sc[:, 4:5], in0=sc[:, 3:4], scalar1=1.0 / 256.0,
                                                                  scalar2=RMS_EPS, op0=ALU.mult, op1=ALU.add),
                                 reads=[scb], writes=[scb])
                            S.op("act", lambda e: e.activation(out=sc[:, 4:5], in_=sc[:, 4:5], func=AF.Sqrt), reads=[scb],
                                 writes=[scb])
                            S.op("dve", lambda e: e.reciprocal(out=sc[:, 5:6], in_=sc[:, 4:5]), reads=[scb], writes=[scb])
                            S.op("dve", lambda e: e.scalar_tensor_tensor(out=a_[:], in0=o_[:], scalar=sc[:, 5:6], in1=Gs[:],
                                                                         op0=ALU.mult, op1=ALU.mult),
                                 reads=[ob_, scb, cb], writes=[abb])
                            pt, pb = self.bank()
                            ptb = pt[:].bitcast(BF16)
                            S.sync("pe", reads=[abb, self.gb], writes=[pb])
                            for e2 in range(2):
                                ins = nc.tensor.transpose(out=ptb[:, e2 * 128:(e2 + 1) * 128], in_=a_[:, e2 * 128:(e2 + 1) * 128],
                                                          identity=self.identb[:])
                            S.done("pe", ins, reads=[abb, self.gb], writes=[pb])
                            S.op("act", lambda e: e.copy(out=at_[:, :, qs * 128:(qs + 1) * 128],
                                                         in_=ptb[:, 0:256].rearrange("p (c t) -> p c t", c=2)),
                                 reads=[pb], writes=[atb])
                        S.dma(self.tile_ap(self.YT[1], tt, 512)[:, 2 * h:2 * h + 2, :], at_[:], reads=[atb],
                              writes=[self.YTb[1][tt]])
            S.barrier()
        self.dump("yt1", self.YT[1], self.YTb[1], BF16)

    def dump_stage(self, nm):
        m = {"proj0": (1, 1), "ffn0": (2, 2), "proj1": (3, 3)}
        if nm in m:
            xi, hi = m[nm]
            self.dump("xt%d" % xi, self.XT[xi], self.XTb[xi], F32)
            self.dump("ht%d" % hi, self.HT[hi], self.HTb[hi], BF16)


def build_program(consts):
    mk = MK()
    nc = mk.build(consts)
    return nc, mk


_CACHE = {}


def kernel(**inputs):
    consts = _host_consts()
    n_cores = 8 if not DEBUG else int(os.environ.get("KCORES", "1"))
    nc, mk = build_program(consts)
    x = np.ascontiguousarray(inputs["x"], dtype=np.float32)
    in_maps = []
    for r in range(n_cores):
        m = {}
        for k, v in inputs.items():
            v = np.asarray(v)
            if k == "x":
                m[k] = np.ascontiguousarray(v[r * NS:(r + 1) * NS])
            elif k == "c":
                m[k] = np.ascontiguousarray(v[r * NS:(r + 1) * NS])
            elif k == "ctx":
                m[k] = np.ascontiguousarray(v[r * NS:(r + 1) * NS])
            else:
                m[k] = v
        for k, v in consts.items():
            m["k_" + k] = v
        in_maps.append(m)
    res = run_bass_kernel_spmd(nc, in_maps, core_ids=list(range(n_cores)))
    if DEBUG:
        _CACHE["res"] = res
    out = np.concatenate([np.asarray(r["out"]) for r in res.results], axis=0)
    return out.astype(np.float32)
```

```python
import os
import math
import numpy as np
import ml_dtypes
import concourse.bass as bass
import concourse.mybir as mybir
from concourse.bass_utils import run_bass_kernel_spmd
from contextlib import ExitStack

F32 = mybir.dt.float32
BF16 = mybir.dt.bfloat16
F32R = mybir.dt.float32r
AF = mybir.ActivationFunctionType
ALU = mybir.AluOpType
AX = mybir.AxisListType

D = 2048
NTOK = 2048
NCTX = 256
FF = 5632
NS = 2
KC = 16
FC = 44
ALPHA = (2.0 * 2) ** 0.25
LN_EPS = 1e-5
RMS_EPS = 1e-5
LAMBDA_INIT1 = 0.8 - 0.6 * math.exp(-0.3 * 1)
SM_SCALE = 1.0 / math.sqrt(128.0)
POOL_WINDOWS = (2, 4, 8, 16)

DEBUG = bool(int(os.environ.get("KDEBUG", "0")))
STOP_AFTER = os.environ.get("KSTOP", "")
DUMPN = os.environ.get("KDUMPN", "").split(",")
NOPOOL = bool(int(os.environ.get("KNOPOOL", "1")))
DUMPS = os.environ.get("KDUMP", "").split(",")


class Buf:
    __slots__ = ("name", "w", "r", "excl")

    def __init__(self, name, excl=False):
        self.name = name
        self.w = None
        self.r = {}
        self.excl = excl


class Sync:
    def __init__(self, nc, stack, n_dma=28):
        self.nc = nc
        self.engs = {"pe": nc.tensor, "act": nc.scalar, "dve": nc.vector, "pool": nc.gpsimd, "sp": nc.sync}
        self.sems = {}
        self.cnt = {}
        for e in self.engs:
            self.sems[e] = stack.enter_context(nc.semaphore("s_" + e))
            self.cnt[e] = 0
        self.n_dma = n_dma
        for i in range(n_dma):
            self.sems[("d", i)] = stack.enter_context(nc.semaphore("s_d%d" % i))
            self.cnt[("d", i)] = 0
        self.rr = 0
        self.known = {e: {} for e in self.engs}
        self.snap = {}
        self.nwaits = 0
        self.ninstr = 0

    def _learn(self, E, ev):
        k = dict(self.known[E])
        sn = self.snap.get(ev)
        if sn:
            for s, v in sn.items():
                if k.get(s, 0) < v:
                    k[s] = v
        if k.get(ev[0], 0) < ev[1]:
            k[ev[0]] = ev[1]
        self.known[E] = k

    def _wait(self, E, s, v):
        if s == E and E in ("pe", "sp"):
            return
        if self.known[E].get(s, 0) >= v:
            return
        self.engs[E].wait_ge(self.sems[s], v)
        self.nwaits += 1
        self._learn(E, (s, v))

    def sync(self, E, reads=(), writes=()):
        need = {}
        for b in reads:
            if b.w is not None:
                s, v = b.w
                if need.get(s, 0) < v:
                    need[s] = v
            if b.excl:
                for s, v in b.r.items():
                    if need.get(s, 0) < v:
                        need[s] = v
        for b in writes:
            if b.w is not None:
                s, v = b.w
                if need.get(s, 0) < v:
                    need[s] = v
            for s, v in b.r.items():
                if need.get(s, 0) < v:
                    need[s] = v
        for s, v in need.items():
            self._wait(E, s, v)

    def done(self, E, ins, reads=(), writes=()):
        self.cnt[E] += 1
        ins.then_inc(self.sems[E], 1)
        ev = (E, self.cnt[E])
        self.snap[ev] = self.known[E]
        for b in writes:
            b.w = ev
            b.r = {}
        for b in reads:
            if b.r.get(E, 0) < ev[1]:
                b.r[E] = ev[1]
        self.ninstr += 1

    def op(self, E, fn, reads=(), writes=()):
        if E == "pool" and NOPOOL:
            E = "dve"
        self.sync(E, reads, writes)
        ins = fn(self.engs[E])
        self.done(E, ins, reads, writes)
        return ins

    def dma(self, out, in_, reads=(), writes=(), Q="sp"):
        i = self.rr
        self.rr = (i + 1) % self.n_dma
        key = ("d", i)
        prev = self.cnt[key]
        if prev:
            self._wait(Q, key, prev)
        self.sync(Q, reads, writes)
        ins = self.engs[Q].dma_start(out=out, in_=in_)
        ins.then_inc(self.sems[key], 16)
        self.cnt[key] += 16
        ev = (key, self.cnt[key])
        self.snap[ev] = self.known[Q]
        for b in writes:
            b.w = ev
            b.r = {}
        for b in reads:
            if b.r.get(key, 0) < ev[1]:
                b.r[key] = ev[1]
        self.ninstr += 1

    def barrier(self, engines=("pe", "act", "dve", "pool", "sp")):
        for E in engines:
            for s, v in self.cnt.items():
                if v:
                    self._wait(E, s, v)


class Stream:
    def __init__(self, S, tiles, loads, ahead=None):
        self.S = S
        self.tiles = tiles
        self.loads = loads
        self.issued = 0
        self.n = len(loads)
        self.ahead = (len(tiles) - 1) if ahead is None else ahead

    def _issue(self, i):
        t, b = self.tiles[i % len(self.tiles)]
        for (o, a, dbufs) in self.loads[i](t):
            self.S.dma(o, a, reads=dbufs, writes=[b])

    def get(self, i):
        ahead = self.ahead
        while self.issued < min(self.n, i + 1 + ahead):
            self._issue(self.issued)
            self.issued += 1
        return self.tiles[i % len(self.tiles)]


def _host_consts():
    c = {}
    c["ident"] = np.eye(128, dtype=np.float32)
    c["identb"] = np.eye(128, dtype=np.float32).astype(ml_dtypes.bfloat16)
    n = np.arange(256)
    ang = 2 * np.pi * np.outer(n, n) / 256.0
    cs = np.concatenate([np.cos(ang), np.sin(ang)], axis=1) / 16.0
    c["dft_c"] = cs.reshape(2, 128, 512).transpose(1, 0, 2).astype(ml_dtypes.bfloat16).copy()
    for N, nm in ((2048, "dft_n"), (256, "dft_x")):
        k = np.arange(N, dtype=np.int64)
        ang = 2 * np.pi * ((np.outer(k, k) % N).astype(np.float64)) / N
        sc = 1.0 / math.sqrt(N)
        Cn = (np.cos(ang) * sc)
        Sn = (-np.sin(ang) * sc)
        kt = min(512, N)
        nch = N // 128
        arr = np.stack([Cn, Sn], axis=0)
        arr = arr.reshape(2, nch, 128, N // kt, kt)
        arr = arr.transpose(3, 2, 1, 0, 4)
        c[nm] = np.ascontiguousarray(arr).astype(ml_dtypes.bfloat16)
    c["dft_n"] = c["dft_n"].reshape(4, 128, -1)
    c["dft_x"] = c["dft_x"].reshape(1, 128, -1)
    for N, nm in ((2048, "efac_n"), (256, "efac_x")):
        t = np.arange(N)
        rows = []
        for w in POOL_WINDOWS:
            lo = -(w // 2)
            hi = w - w // 2
            cnt = np.minimum(t + hi, N) - np.maximum(t + lo, 0)
            f = w / cnt
            rows.append(np.concatenate([f[:8], f[-8:]]))
        c[nm] = np.stack(rows).astype(np.float32)
    rows_ = np.repeat(np.arange(32), 64).astype(np.float32)
    cols_ = np.tile(np.arange(64), 32).astype(np.float32)
    inv = (np.float32(10000.0) ** (-np.arange(0, 64, 2, dtype=np.float32) / np.float32(64.0))).astype(np.float32)
    ang_r = (rows_[:, None] * inv[None, :]).astype(np.float32)
    ang_c = (cols_[:, None] * inv[None, :]).astype(np.float32)
    c["rope_cos"] = np.concatenate([np.cos(ang_r), np.cos(ang_r), np.cos(ang_c), np.cos(ang_c)], axis=1).astype(np.float32)
    c["rope_sin"] = np.concatenate([-np.sin(ang_r), np.sin(ang_r), -np.sin(ang_c), np.sin(ang_c)], axis=1).astype(np.float32)
    return c


CONST_SPECS = None


class MK:
    def __init__(self):
        self.nc = bass.Bass("TRN2", target_bir_lowering=False)
        self.stack = ExitStack()
        self.dbg = {}

    def din(self, name, shape, dt=F32):
        return self.nc.dram_tensor(name, list(shape), dt, kind="ExternalInput").ap()

    def dscratch(self, name, shape, dt):
        return self.nc.dram_tensor(name, list(shape), dt).ap()

    def dout(self, name, shape, dt=F32):
        return self.nc.dram_tensor(name, list(shape), dt, kind="ExternalOutput").ap()

    def sb(self, st, name, shape, dt):
        t = st.enter_context(self.nc.sbuf_tensor(name, list(shape), dt))
        return t

    def build(self, consts):
        nc = self.nc
        st = self.stack
        S = self.S = Sync(nc, st)
        self.uid = 0
        I = self.I = {}
        I["x"] = self.din("x", [NS, NTOK, D])
        I["c"] = self.din("c", [NS, D])
        I["ctx"] = self.din("ctx", [NS, NCTX, D])
        I["c_ctx"] = self.din("c_ctx", [D])
        I["w_mod"] = self.din("w_mod", [2, D, 6 * D])
        I["b_mod"] = self.din("b_mod", [2, 6 * D])
        for nm in ("ln1_g", "ln1_b", "ln2_g", "ln2_b"):
            I[nm] = self.din(nm, [2, D])
        I["ffn_w_gate"] = self.din("ffn_w_gate", [2, D, FF])
        I["ffn_w_up"] = self.din("ffn_w_up", [2, D, FF])
        I["ffn_w_down"] = self.din("ffn_w_down", [2, FF, D])
        I["pf_w_in"] = self.din("pf_w_in", [1, D, D])
        I["pool_lin"] = self.din("pool_lin", [1, 4, 256, 256])
        I["pool_scale"] = self.din("pool_scale", [1, 1024])
        I["fourier_lin"] = self.din("fourier_lin", [1, 4, 256, 256])
        I["pf_w_out"] = self.din("pf_w_out", [1, D, D])
        I["da_w_in"] = self.din("da_w_in", [1, D, 3 * D])
        for nm in ("da_lam_q1", "da_lam_k1", "da_lam_q2", "da_lam_k2"):
            I[nm] = self.din(nm, [1, 128])
        I["da_subln_g"] = self.din("da_subln_g", [1, 256])
        I["da_w_out"] = self.din("da_w_out", [1, D, D])
        Cst = self.C = {}
        for k, v in consts.items():
            dt = BF16 if v.dtype == ml_dtypes.bfloat16 else F32
            Cst[k] = self.din("k_" + k, v.shape, dt)
        self.out = self.dout("out", [NS, NTOK, D])

        W = self.W = {}
        W["win"] = self.dscratch("w_win", [4, 128, 16 * 512], BF16)
        W["wout0"] = self.dscratch("w_wout0", [4, 128, 16 * 512], BF16)
        W["wout1"] = self.dscratch("w_wout1", [4, 128, 16 * 512], BF16)
        W["plin"] = self.dscratch("w_plin", [4, 128, 2 * 256], BF16)
        W["flin"] = self.dscratch("w_flin", [4, 128, 2 * 256], BF16)
        W["wg"] = self.dscratch("w_wg", [2, 22, 128, 16 * 256], BF16)
        W["wu"] = self.dscratch("w_wu", [2, 22, 128, 16 * 256], BF16)
        W["wd"] = self.dscratch("w_wd", [2, 16, 128, 44 * 128], BF16)
        W["wa"] = self.dscratch("w_wa", [8, 3, 128, 16 * 256], BF16)
        self.Wb = {}
        NT = NS * 5
        self.XT = [self.dscratch("xt%d" % i, [NT, 128, 16 * 512], F32) for i in range(4)]
        self.HT = [self.dscratch("ht%d" % i, [NT, 128, 16 * 512], BF16) for i in range(4)]
        self.YT = [self.dscratch("yt%d" % i, [NT, 128, 16 * 512], BF16) for i in range(2)]
        self.PQ = self.dscratch("pq", [NS, 18, 128, 4 * 512], BF16)
        self.XTb = [[Buf("xt%d_%d" % (i, t)) for t in range(NT)] for i in range(4)]
        self.HTb = [[Buf("ht%d_%d" % (i, t)) for t in range(NT)] for i in range(4)]
        self.YTb = [[Buf("yt%d_%d" % (i, t)) for t in range(NT)] for i in range(2)]
        self.PQb = [[Buf("pq%d_%d" % (s, t)) for t in range(18)] for s in range(NS)]

        self.ident = self.sb(st, "ident", [128, 128], F32)
        self.identb = self.sb(st, "identb", [128, 128], BF16)
        self.ones_f = self.sb(st, "ones_f", [128, 128], F32)
        self.ones_b = self.sb(st, "ones_b", [128, 128], BF16)
        self.modT = [self.sb(st, "modT%d" % l, [128, 96, 3], F32) for l in range(2)]
        self.vecT = self.sb(st, "vecT", [128, 9, 16], F32)
        self.gb = Buf("globals")
        self.ps = []
        self.psb = []
        for i in range(8):
            t = st.enter_context(nc.psum_tensor("ps%d" % i, [128, 512], F32))
            self.ps.append(t)
            self.psb.append(Buf("ps%d" % i, excl=True))
        self.ps_rr = 0
        self.nbanks = 8
        self.outb = Buf('out')

        S.dma(self.ident[:], self.C["ident"][:, :], writes=[self.gb])
        S.dma(self.identb[:], self.C["identb"][:, :], writes=[self.gb])
        S.op("dve", lambda e: e.memset(self.ones_f[:], 1.0 / D), writes=[self.gb])
        S.op("dve", lambda e: e.memset(self.ones_b[:], 1.0), writes=[self.gb])

        self.tiles = []
        for s in range(NS):
            for j in range(4):
                self.tiles.append((s, j, 512))
            self.tiles.append((s, 4, 256))

        lat = [s * 5 + j for s in range(NS) for j in range(4)]
        allt = list(range(NS * 5))
        steps = [
            ("mod", lambda: self.phase_mod()),
            ("convert", lambda: self.phase_convert()),
            ("xin", lambda: self.phase_xin()),
            ("m1a", lambda: self.phase_m1a()),
            ("m1b", lambda: self.phase_m1b()),
            ("m3", lambda: self.phase_m3()),
            ("proj0", lambda: self.phase_proj_ep("wout0", self.W["wout0"], 0, 0, 0, 1, 1, allt)),
            ("ffn0", lambda: self.phase_ffn(0, 1, 1, 2, 2, allt, False)),
            ("attn", lambda: self.phase_attn()),
            ("proj1", lambda: self.phase_proj_ep("wout1", self.W["wout1"], 1, 1, 2, 3, 3, lat)),
            ("ffn1", lambda: self.phase_ffn(1, 3, 3, None, None, lat, True)),
        ]
        for nm, fn in steps:
            fn()
            if DEBUG and nm in DUMPS:
                self.dump_stage(nm)
            if STOP_AFTER == nm:
                break
        return self.finish()

    def finish(self):
        S = self.S
        S.barrier()
        self.stack.close()
        return self.nc

    def bank(self):
        i = self.ps_rr % self.nbanks
        self.ps_rr = (i + 1) % self.nbanks
        return self.ps[i], self.psb[i]

    def name(self, p):
        self.uid += 1
        return "%s_%d" % (p, self.uid)

    def tile_ap(self, t, tt, ntok, nch=16, full=512):
        return t[tt].rearrange("p (c n) -> p c n", c=nch)[:, :, 0:ntok]

    def phase_mod(self):
        nc, S, I = self.nc, self.S, self.I
        with ExitStack() as st:
            c_sb = self.sb(st, "c_sb", [3, D], F32)
            cs = self.sb(st, "cs", [3, D], F32)
            csT = self.sb(st, "csT", [128, 16, 3], F32)
            bm = self.sb(st, "bm", [3, 6 * D], F32)
            rows = self.sb(st, "modrows", [3, 6 * D], F32)
            vrows = self.sb(st, "vrows", [16, 9, 128], F32)
            wt = [self.sb(st, "wmt%d" % i, [128, 16, 512], F32) for i in range(2)]
            wtb = [Buf("wmt%d" % i) for i in range(2)]
            b_c, b_cs, b_csT, b_bm, b_rows, b_vr = (Buf(n) for n in ("c", "cs", "csT", "bm", "rows", "vr"))
            S.dma(c_sb[0:2, :], I["c"][:, :], writes=[b_c])
            S.dma(c_sb[2:3, :], I["c_ctx"].rearrange("(o d) -> o d", o=1), writes=[b_c])
            vecs = [I["ln1_g"][0], I["ln1_b"][0], I["ln2_g"][0], I["ln2_b"][0],
                    I["ln1_g"][1], I["ln1_b"][1], I["ln2_g"][1], I["ln2_b"][1]]
            S.op("dve", lambda e: e.memset(vrows[:], 0.0), writes=[b_vr])
            for v, ap in enumerate(vecs):
                S.dma(vrows[0:16, v, :], ap.rearrange("(c p) -> c p", p=128), writes=[b_vr])
            S.dma(vrows[0:8, 8, :], I["pool_scale"][0].rearrange("(c p) -> c p", p=128), writes=[b_vr])
            S.op("act", lambda e: e.activation(out=cs[:], in_=c_sb[:], func=AF.Silu), reads=[b_c], writes=[b_cs])
            pt, pb = self.bank()
            S.sync("pe", reads=[b_cs, self.gb], writes=[pb])
            for k in range(16):
                ins = nc.tensor.transpose(out=pt[:, k * 3:(k + 1) * 3], in_=cs[0:3, k * 128:(k + 1) * 128],
                                          identity=self.ident[0:3, 0:3])
            S.done("pe", ins, reads=[b_cs, self.gb], writes=[pb])
            S.op("dve", lambda e: e.tensor_copy(out=csT[:].rearrange("p k s -> p (k s)"), in_=pt[:, 0:48]),
                 reads=[pb], writes=[b_csT])
            pt, pb = self.bank()
            S.sync("pe", reads=[b_vr, self.gb], writes=[pb])
            for v in range(9):
                ins = nc.tensor.transpose(out=pt[:, v * 16:(v + 1) * 16], in_=vrows[0:16, v, :],
                                          identity=self.ident[0:16, 0:16])
            S.done("pe", ins, reads=[b_vr, self.gb], writes=[pb])
            S.op("dve", lambda e: e.tensor_copy(out=self.vecT[:].rearrange("p v c -> p (v c)"), in_=pt[:, 0:144]),
                 reads=[pb], writes=[self.gb])
            for l in range(2):
                S.dma(bm[:], I["b_mod"][l].rearrange("(o d) -> o d", o=1).broadcast_to([3, 6 * D]), writes=[b_bm],
                      reads=[b_rows])
                wsrc = I["w_mod"][l].rearrange("(k p) n -> p k n", p=128)

                def mk(nb):
                    return lambda t: [(t[:], wsrc[:, :, nb * 512:(nb + 1) * 512], [])]
                strm = Stream(S, list(zip(wt, wtb)), [mk(nb) for nb in range(24)])
                for nb in range(24):
                    t, tb = strm.get(nb)
                    pt, pb = self.bank()
                    S.sync("pe", reads=[tb, b_csT], writes=[pb])
                    for k in range(16):
                        ins = nc.tensor.matmul(pt[0:3, :], lhsT=csT[:, k, :], rhs=t[:, k, :],
                                               start=(k == 0), stop=(k == 15))
                    S.done("pe", ins, reads=[tb, b_csT], writes=[pb])
                    S.op("dve", lambda e: e.tensor_tensor(out=rows[0:3, nb * 512:(nb + 1) * 512], in0=pt[0:3, :],
                                                          in1=bm[0:3, nb * 512:(nb + 1) * 512], op=ALU.add),
                         reads=[pb, b_bm], writes=[b_rows])
                pt, pb = self.bank()
                S.sync("pe", reads=[b_rows, self.gb], writes=[pb])
                for blk in range(96):
                    ins = nc.tensor.transpose(out=pt[:, blk * 3:(blk + 1) * 3], in_=rows[0:3, blk * 128:(blk + 1) * 128],
                                              identity=self.ident[0:3, 0:3])
                S.done("pe", ins, reads=[b_rows, self.gb], writes=[pb])
                S.op("dve", lambda e: e.tensor_copy(out=self.modT[l][:].rearrange("p v s -> p (v s)"), in_=pt[:, 0:288]),
                     reads=[pb], writes=[self.gb])
                if DEBUG:
                    d = self.dout("dbg_modrows%d" % l, [3, 6 * D])
                    S.dma(d[:, :], rows[:], reads=[b_rows])
            if DEBUG:
                d = self.dout("dbg_modT", [2, 128, 96 * 3])
                for l in range(2):
                    S.dma(d[l], self.modT[l][:].rearrange("p v s -> p (v s)"), reads=[self.gb])
                d = self.dout("dbg_vecT", [128, 9 * 16])
                S.dma(d[:, :], self.vecT[:].rearrange("p v c -> p (v c)"), reads=[self.gb])
            S.barrier()

    def modv(self, l, v, s):
        return self.modT[l][:, v * 16:(v + 1) * 16, s]

    def dump(self, name, ap, bufs, dt):
        if not DEBUG or name not in DUMPN:
            return
        d = self.dout("dbg_" + name, list(ap.shape), dt)
        if len(ap.shape) == 3:
            for i in range(ap.shape[0]):
                self.S.dma(d[i], ap[i], reads=bufs)
        else:
            self.S.dma(d, ap, reads=bufs)

    def phase_convert(self):
        nc, S, I, W = self.nc, self.S, self.I, self.W
        jobs = []

        def add(src, kc, cw, dst, key):
            jobs.append((src, kc, cw, [dst], [key]))
            self.Wb[key] = Buf("w" + str(key))

        def add2(src, dsts, keys):
            jobs.append((src, 16, 512, dsts, keys))
            for key in keys:
                self.Wb[key] = Buf("w" + str(key))

        win = I["pf_w_in"][0].rearrange("(k p) n -> p k n", p=128)
        for b in range(4):
            add(win[:, :, b * 512:(b + 1) * 512], 16, 512, W["win"][b], ("win", b))
        for g in range(4):
            add(I["pool_lin"][0][g].rearrange("(c p) d -> p c d", p=128), 2, 256, W["plin"][g], ("plin", g))
            add(I["fourier_lin"][0][g].rearrange("(c p) d -> p c d", p=128), 2, 256, W["flin"][g], ("flin", g))
        wo = I["pf_w_out"][0].rearrange("(k p) n -> p k n", p=128)
        for b in range(4):
            add(wo[:, :, b * 512:(b + 1) * 512], 16, 512, W["wout0"][b], ("wout0", b))
        for l in range(2):
            wg = I["ffn_w_gate"][l].rearrange("(k p) n -> p k n", p=128)
            wu = I["ffn_w_up"][l].rearrange("(k p) n -> p k n", p=128)
            wd = I["ffn_w_down"][l].rearrange("(f p) d -> p f d", p=128)
            for b in range(0, 22, 2):
                add2(wg[:, :, b * 256:(b + 2) * 256], [W["wg"][l, b], W["wg"][l, b + 1]], [("wg", l, b), ("wg", l, b + 1)])
                add2(wu[:, :, b * 256:(b + 2) * 256], [W["wu"][l, b], W["wu"][l, b + 1]], [("wu", l, b), ("wu", l, b + 1)])
            for b in range(16):
                add(wd[:, :, b * 128:(b + 1) * 128], 44, 128, W["wd"][l, b], ("wd", l, b))
            if l == 0:
                wa = I["da_w_in"][0].rearrange("(k p) n -> p k n", p=128)
                for h in range(0, 8, 2):
                    for part in range(3):
                        c0 = part * 2048 + h * 256
                        add2(wa[:, :, c0:c0 + 512], [W["wa"][h, part], W["wa"][h + 1, part]],
                             [("wa", h, part), ("wa", h + 1, part)])
                wo = I["da_w_out"][0].rearrange("(k p) n -> p k n", p=128)
                for b in range(4):
                    add(wo[:, :, b * 512:(b + 1) * 512], 16, 512, W["wout1"][b], ("wout1", b))
        with ExitStack() as st:
            stg = [(self.sb(st, "cst%d" % i, [128, 8192], F32), Buf("cst%d" % i)) for i in range(3)]
            bfs = [(self.sb(st, "cbf%d" % i, [128, 8192], BF16), (Buf("cbfa%d" % i), Buf("cbfb%d" % i))) for i in range(3)]

            def mkload(src, kc, cw):
                def f(t):
                    t3 = t[:, 0:kc * cw].rearrange("p (k c) -> p k c", k=kc)
                    step = max(1, 1536 // 128) if cw < 512 else kc
                    step = min(step, kc)
                    return [(t3[:, k0:min(kc, k0 + step), :], src[:, k0:min(kc, k0 + step), :], [])
                            for k0 in range(0, kc, step)]
                return f
            engs = os.environ.get("KENG", "dve,act").split(",")
            jobs = jobs[:int(os.environ.get("KJOBS", "100000"))]
            strm = Stream(S, stg, [mkload(j[0], j[1], j[2]) for j in jobs])
            for i, (src, kc, cw, dsts, keys) in enumerate(jobs):
                t, tb = strm.get(i)
                o, (oba, obb) = bfs[i % 3]
                n = kc * cw
                if len(dsts) == 1:
                    e = engs[i % len(engs)]
                    if e == "act":
                        S.op("act", lambda en: en.copy(out=o[:, 0:n], in_=t[:, 0:n]), reads=[tb], writes=[oba, obb])
                    else:
                        S.op(e, lambda en: en.tensor_copy(out=o[:, 0:n], in_=t[:, 0:n]), reads=[tb], writes=[oba, obb])
                    S.dma(dsts[0], o[:, 0:n], reads=[oba, obb], writes=[self.Wb[keys[0]]])
                else:
                    t3 = t[:, 0:8192].rearrange("p (k c) -> p k c", k=16)
                    for hf in range(2):
                        e = engs[(i + hf) % len(engs)]
                        ohalf = o[:, hf * 4096:(hf + 1) * 4096]
                        o3 = ohalf.rearrange("p (k c) -> p k c", k=16)
                        ob = (oba, obb)[hf]
                        if e == "act":
                            S.op("act", lambda en: en.copy(out=o3, in_=t3[:, :, hf * 256:(hf + 1) * 256]), reads=[tb],
                                 writes=[ob])
                        else:
                            S.op(e, lambda en: en.tensor_copy(out=o3, in_=t3[:, :, hf * 256:(hf + 1) * 256]), reads=[tb],
                                 writes=[ob])
                        S.dma(dsts[hf], ohalf, reads=[ob], writes=[self.Wb[keys[hf]]])
            S.barrier()

    def wload(self, key, dram_ap):
        return lambda t: [(t[:, 0:dram_ap.shape[-1]], dram_ap, [self.Wb[key]])]

    def phase_xin(self):
        nc, S, I = self.nc, self.S, self.I
        with ExitStack() as st:
            xin = [(self.sb(st, "xin%d" % i, [128, D], F32), Buf("xin%d" % i)) for i in range(3)]
            xts = [(self.sb(st, "xts%d" % i, [128, 16, 512], F32), Buf("xts%d" % i)) for i in range(2)]
            hts = [(self.sb(st, "hts%d" % i, [128, 16, 512], BF16), Buf("hts%d" % i)) for i in range(2)]
            A = self.sb(st, "xinA", [128, 3, 16], F32)
            vb = Buf("xinA")
            for s in range(3):
                S.op("dve", lambda e: e.tensor_scalar(out=A[:, s, :], in0=self.modv(0, 1, s), scalar1=1.0, scalar2=None,
                                                      op0=ALU.add), reads=[self.gb], writes=[vb])
            loads = []
            for tt, (s, j, ntok) in enumerate(self.tiles):
                for sub in range(ntok // 128):
                    if j < 4:
                        src = I["x"][s, j * 512 + sub * 128: j * 512 + (sub + 1) * 128, :]
                    else:
                        src = I["ctx"][s, sub * 128:(sub + 1) * 128, :]
                    loads.append((lambda src: (lambda t: [(t[:], src, [])]))(src))
            strm = Stream(S, xin, loads)
            idx = 0
            for tt, (s, j, ntok) in enumerate(self.tiles):
                src_s = s if j < 4 else 2
                xt, xb = xts[tt % 2]
                ht, hb = hts[tt % 2]
                for sub in range(ntok // 128):
                    t, tb = strm.get(idx)
                    idx += 1
                    for b in range(4):
                        pt, pb = self.bank()
                        S.sync("pe", reads=[tb, self.gb], writes=[pb])
                        for q in range(4):
                            c = 4 * b + q
                            ins = nc.tensor.transpose(out=pt[:, q * 128:(q + 1) * 128], in_=t[:, c * 128:(c + 1) * 128],
                                                      identity=self.ident[:])
                        S.done("pe", ins, reads=[tb, self.gb], writes=[pb])
                        S.op("act", lambda e: e.mul(out=xt[:, 4 * b:4 * b + 4, sub * 128:(sub + 1) * 128],
                                                    in_=pt[:, 0:512].rearrange("p (q n) -> p q n", q=4), mul=ALPHA),
                             reads=[pb], writes=[xb])
                        for q in range(4):
                            c = 4 * b + q
                            S.op("dve", lambda e: e.tensor_scalar(
                                out=ht[:, c, sub * 128:(sub + 1) * 128], in0=pt[:, q * 128:(q + 1) * 128],
                                scalar1=A[:, src_s, c:c + 1], scalar2=self.modT[0][:, 0 * 16 + c:0 * 16 + c + 1, src_s],
                                op0=ALU.mult, op1=ALU.add), reads=[pb, vb, self.gb], writes=[hb])
                S.dma(self.tile_ap(self.XT[0], tt, ntok), xt[:, :, 0:ntok], reads=[xb], writes=[self.XTb[0][tt]])
                S.dma(self.tile_ap(self.HT[0], tt, ntok), ht[:, :, 0:ntok], reads=[hb], writes=[self.HTb[0][tt]])
            S.barrier()
        self.dump("xt0", self.XT[0], sum(self.XTb[0:1], []), F32)
        self.dump("ht0", self.HT[0], sum(self.HTb[0:1], []), BF16)

    def ep_vecs(self, st, l, stage, final):
        S = self.S
        V = self.sb(st, self.name("epv"), [128, 3, 5, 16], F32)
        vb = Buf("epv")
        lg = self.vecT[:, 4 * l + 2 * (stage - 1), :]
        lb = self.vecT[:, 4 * l + 2 * (stage - 1) + 1, :]
        for s in range(3):
            gv = self.modv(l, 2 if stage == 1 else 5, s)
            S.op("dve", lambda e: e.tensor_copy(out=V[:, s, 0, :], in_=gv), reads=[self.gb], writes=[vb])
            if not final:
                if stage == 1:
                    sc, sh = self.modv(l, 4, s), self.modv(l, 3, s)
                else:
                    sc, sh = self.modv(l + 1, 1, s), self.modv(l + 1, 0, s)
                S.op("dve", lambda e: e.scalar_tensor_tensor(out=V[:, s, 1, :], in0=sc, scalar=1.0, in1=lg,
                                                             op0=ALU.add, op1=ALU.mult), reads=[self.gb], writes=[vb])
                S.op("dve", lambda e: e.scalar_tensor_tensor(out=V[:, s, 2, :], in0=sc, scalar=1.0, in1=lb,
                                                             op0=ALU.add, op1=ALU.mult), reads=[self.gb], writes=[vb])
                S.op("dve", lambda e: e.tensor_tensor(out=V[:, s, 2, :], in0=V[:, s, 2, :], in1=sh, op=ALU.add),
                     reads=[self.gb], writes=[vb])
            a = 1.0 if final else ALPHA
            S.op("dve", lambda e: e.tensor_scalar(out=V[:, s, 3, :], in0=lg, scalar1=a, scalar2=None, op0=ALU.mult),
                 reads=[self.gb], writes=[vb])
            S.op("dve", lambda e: e.tensor_scalar(out=V[:, s, 4, :], in0=lb, scalar1=a, scalar2=None, op0=ALU.mult),
                 reads=[self.gb], writes=[vb])
        return V, vb

    class Epi:
        def __init__(self, mk, st, l, stage, final, nbuf_y, nbuf_h, xt_in, xt_out, ht_out):
            self.mk = mk
            self.final = final
            self.V, self.vb = mk.ep_vecs(st, l, stage, final)
            self.ys = [(mk.sb(st, mk.name("epy"), [128, 16, 512], F32), [Buf("epy%d" % c) for c in range(16)])
                       for i in range(nbuf_y)]
            if not final:
                self.hs = [(mk.sb(st, mk.name("eph"), [128, 16, 512], BF16), [Buf("eph%d" % c) for c in range(16)])
                           for i in range(nbuf_h)]
            else:
                self.os = [(mk.sb(st, mk.name("epo"), [128, D], F32), Buf("epo")) for i in range(2)]
                self.on = 0
            self.sq = [(mk.sb(st, mk.name("epsq"), [128, 512], F32), Buf("epsq")) for i in range(3)]
            self.mean = mk.sb(st, mk.name("epm"), [128, 512], F32)
            self.rstd = mk.sb(st, mk.name("epr"), [128, 512], F32)
            self.mr = mk.sb(st, mk.name("epmr"), [128, 512], F32)
            self.stb = Buf("epstat")
            self.xt_in, self.xt_out, self.ht_out = xt_in, xt_out, ht_out
            self.n = 0
            self.sqn = 0
            self.pending = None

        def begin(self, tt, ntok, src_s):
            mk, S = self.mk, self.mk.S
            self.tt, self.ntok, self.src_s = tt, ntok, src_s
            self.y, self.yb = self.ys[self.n % len(self.ys)]
            if not self.final:
                self.h, self.hb = self.hs[self.n % len(self.hs)]
            self.n += 1
            S.dma(self.y[:, :, 0:ntok], mk.tile_ap(mk.XT[self.xt_in], tt, ntok), reads=[mk.XTb[self.xt_in][tt]],
                  writes=self.yb)

        def _stats(self, c, sq, sqb, last):
            mk, S, nc = self.mk, self.mk.S, self.mk.nc
            nt = self.ntok
            S.sync("pe", reads=[self.yb[c], sqb, mk.gb], writes=[mk.psb[6], mk.psb[7]])
            nc.tensor.matmul(mk.ps[6][:, 0:nt], lhsT=mk.ones_f[:], rhs=self.y[:, c, 0:nt],
                             start=(c == 0), stop=last)
            ins = nc.tensor.matmul(mk.ps[7][:, 0:nt], lhsT=mk.ones_f[:], rhs=sq[:, 0:nt],
                                   start=(c == 0), stop=last)
            S.done("pe", ins, reads=[self.yb[c], sqb, mk.gb], writes=[mk.psb[6], mk.psb[7]])

        def flush(self):
            if self.pending is not None:
                self._stats(*self.pending)
                self.pending = None

        def chunk(self, c, pt, pb):
            mk, S = self.mk, self.mk.S
            nt = self.ntok
            self.flush()
            y, yb = self.y, self.yb
            G = self.V[:, self.src_s, 0, c:c + 1]
            S.op("dve", lambda e: e.scalar_tensor_tensor(out=y[:, c, 0:nt], in0=pt[:, 0:nt], scalar=G, in1=y[:, c, 0:nt],
                                                         op0=ALU.mult, op1=ALU.add),
                 reads=[pb, self.vb, yb[c]], writes=[yb[c]])
            sq, sqb = self.sq[self.sqn % 3]
            self.sqn += 1
            S.op("act", lambda e: e.activation(out=sq[:, 0:nt], in_=y[:, c, 0:nt], func=AF.Square), reads=[yb[c]],
                 writes=[sqb])
            self.pending = (c, sq, sqb, c == 15)

        def finish(self):
            mk, S, nc = self.mk, self.mk.S, self.mk.nc
            nt, tt, s = self.ntok, self.tt, self.src_s
            self.flush()
            y, yb = self.y, self.yb
            mean, rstd, mr = self.mean, self.rstd, self.mr
            S.op("act", lambda e: e.copy(out=mean[:, 0:nt], in_=mk.ps[6][:, 0:nt]), reads=[mk.psb[6]], writes=[self.stb])
            S.op("dve", lambda e: e.tensor_tensor(out=rstd[:, 0:nt], in0=mean[:, 0:nt], in1=mean[:, 0:nt], op=ALU.mult),
                 reads=[self.stb], writes=[self.stb])
            S.op("dve", lambda e: e.tensor_tensor(out=rstd[:, 0:nt], in0=mk.ps[7][:, 0:nt], in1=rstd[:, 0:nt],
                                                  op=ALU.subtract), reads=[mk.psb[7], self.stb], writes=[self.stb])
            S.op("dve", lambda e: e.tensor_scalar(out=rstd[:, 0:nt], in0=rstd[:, 0:nt], scalar1=LN_EPS, scalar2=None,
                                                  op0=ALU.add), reads=[self.stb], writes=[self.stb])
            S.op("act", lambda e: e.activation(out=rstd[:, 0:nt], in_=rstd[:, 0:nt], func=AF.Sqrt), reads=[self.stb],
                 writes=[self.stb])
            S.op("dve", lambda e: e.reciprocal(out=rstd[:, 0:nt], in_=rstd[:, 0:nt]), reads=[self.stb], writes=[self.stb])
            S.op("dve", lambda e: e.tensor_tensor(out=mr[:, 0:nt], in0=mean[:, 0:nt], in1=rstd[:, 0:nt], op=ALU.mult),
                 reads=[self.stb], writes=[self.stb])
            V = self.V
            for c in range(16):
                S.op("pool", lambda e: e.tensor_tensor(out=y[:, c, 0:nt], in0=y[:, c, 0:nt], in1=rstd[:, 0:nt], op=ALU.mult),
                     reads=[self.stb, yb[c]], writes=[yb[c]])
                S.op("pool", lambda e: e.tensor_tensor(out=y[:, c, 0:nt], in0=y[:, c, 0:nt], in1=mr[:, 0:nt],
                                                       op=ALU.subtract), reads=[self.stb, yb[c]], writes=[yb[c]])
                if not self.final:
                    h, hb = self.h, self.hb
                    S.op("act", lambda e: e.activation(out=h[:, c, 0:nt], in_=y[:, c, 0:nt], func=AF.Identity,
                                                       scale=V[:, s, 1, c:c + 1], bias=V[:, s, 2, c:c + 1]),
                         reads=[yb[c], self.vb], writes=[hb[c]])
                S.op("dve", lambda e: e.tensor_scalar(out=y[:, c, 0:nt], in0=y[:, c, 0:nt], scalar1=V[:, s, 3, c:c + 1],
                                                      scalar2=V[:, s, 4, c:c + 1], op0=ALU.mult, op1=ALU.add),
                     reads=[yb[c], self.vb], writes=[yb[c]])
            if not self.final:
                S.dma(mk.tile_ap(mk.XT[self.xt_out], tt, nt), y[:, :, 0:nt], reads=yb, writes=[mk.XTb[self.xt_out][tt]])
                if self.ht_out is not None:
                    S.dma(mk.tile_ap(mk.HT[self.ht_out], tt, nt), self.h[:, :, 0:nt], reads=self.hb,
                          writes=[mk.HTb[self.ht_out][tt]])
            else:
                sidx, j, _ = mk.tiles[tt]
                for sub in range(nt // 128):
                    o, ob = self.os[self.on % 2]
                    self.on += 1
                    for b in range(4):
                        pt, pb = mk.bank()
                        S.sync("pe", reads=yb[4 * b:4 * b + 4] + [mk.gb], writes=[pb])
                        for q in range(4):
                            c = 4 * b + q
                            ins = nc.tensor.transpose(out=pt[:, q * 128:(q + 1) * 128],
                                                      in_=y[:, c, sub * 128:(sub + 1) * 128], identity=mk.ident[:])
                        S.done("pe", ins, reads=yb[4 * b:4 * b + 4] + [mk.gb], writes=[pb])
                        if b % 2 == 0:
                            S.op("act", lambda e: e.copy(out=o[:, b * 512:(b + 1) * 512], in_=pt[:, 0:512]), reads=[pb],
                                 writes=[ob])
                        else:
                            S.op("dve", lambda e: e.tensor_copy(out=o[:, b * 512:(b + 1) * 512], in_=pt[:, 0:512]),
                                 reads=[pb], writes=[ob])
                    t0 = j * 512 + sub * 128
                    S.dma(mk.out[sidx, t0:t0 + 128, :], o[:], reads=[ob], writes=[mk.outb])

    def phase_proj_ep(self, wkey, wdram, yt_idx, l, xt_in, xt_out, ht_out, tile_ids):
        nc, S = self.nc, self.S
        with ExitStack() as st:
            wt = self.sb(st, self.name("pw"), [128, 16, 4, 512], BF16)
            wb = Buf("pw")
            for b in range(4):
                S.dma(wt[:, :, b, :], wdram[b].rearrange("p (k c) -> p k c", k=16), reads=[self.Wb[(wkey, b)]], writes=[wb])
            ins_t = [(self.sb(st, self.name("pin"), [128, 16, 512], BF16), Buf("pin")) for i in range(2)]
            ep = MK.Epi(self, st, l, 1, False, 2, 1, xt_in, xt_out, ht_out)
            loads = []
            for tt in tile_ids:
                s, j, ntok = self.tiles[tt]
                loads.append((lambda tt, ntok: (lambda t: [(t[:, :, 0:ntok], self.tile_ap(self.YT[yt_idx], tt, ntok),
                                                            [self.YTb[yt_idx][tt]])]))(tt, ntok))
            strm = Stream(S, ins_t, loads)
            self.nbanks = 6
            for i, tt in enumerate(tile_ids):
                s, j, ntok = self.tiles[tt]
                src_s = s if j < 4 else 2
                ep.begin(tt, ntok, src_s)
                t, tb = strm.get(i)
                for c in range(16):
                    pt, pb = self.bank()
                    S.sync("pe", reads=[tb, wb], writes=[pb])
                    for k in range(16):
                        ins = nc.tensor.matmul(pt[:, 0:ntok], lhsT=wt[:, k, c // 4, (c % 4) * 128:(c % 4 + 1) * 128],
                                               rhs=t[:, k, 0:ntok], start=(k == 0), stop=(k == 15))
                    S.done("pe", ins, reads=[tb, wb], writes=[pb])
                    ep.chunk(c, pt, pb)
                ep.finish()
            self.nbanks = 8
            S.barrier()

    def phase_ffn(self, l, ht_in, xt_in, xt_out, ht_out, tile_ids, final):
        nc, S, W = self.nc, self.S, self.W
        with ExitStack() as st:
            hin = [(self.sb(st, self.name("fh"), [128, 16, 512], BF16), Buf("fh")) for i in range(2)]
            actb = self.sb(st, self.name("fact"), [128, FC, 512], BF16)
            actbuf = [Buf("fact%d" % f) for f in range(FC)]
            wgs = [(self.sb(st, self.name("fwg"), [128, 16 * 256], BF16), Buf("fwg")) for i in range(2)]
            wus = [(self.sb(st, self.name("fwu"), [128, 16 * 256], BF16), Buf("fwu")) for i in range(2)]
            wds = [(self.sb(st, self.name("fwd"), [128, 44 * 128], BF16), Buf("fwd")) for i in range(2)]
            sgs = [(self.sb(st, self.name("fsg"), [128, 512], F32), Buf("fsg")) for i in range(3)]
            ep = MK.Epi(self, st, l, 2, final, 1, 1, xt_in, xt_out, ht_out)
            nt_ = len(tile_ids)
            hloads, gl, ul, dl = [], [], [], []
            for tt in tile_ids:
                s, j, ntok = self.tiles[tt]
                hloads.append((lambda tt, ntok: (lambda t: [(t[:, :, 0:ntok], self.tile_ap(self.HT[ht_in], tt, ntok),
                                                             [self.HTb[ht_in][tt]])]))(tt, ntok))
                for b in range(22):
                    gl.append(self.wload(("wg", l, b), W["wg"][l, b]))
                    ul.append(self.wload(("wu", l, b), W["wu"][l, b]))
                for b in range(16):
                    dl.append(self.wload(("wd", l, b), W["wd"][l, b]))
            hs = Stream(S, hin, hloads)
            gs = Stream(S, wgs, gl)
            us = Stream(S, wus, ul)
            ds = Stream(S, wds, dl)
            self.nbanks = 6
            sgn = 0
            for i, tt in enumerate(tile_ids):
                s, j, ntok = self.tiles[tt]
                src_s = s if j < 4 else 2
                h, hb = hs.get(i)
                for b in range(22):
                    wg, wgb = gs.get(i * 22 + b)
                    wu, wub = us.get(i * 22 + b)
                    wg3 = wg[:].rearrange("p (k c) -> p k c", k=16)
                    wu3 = wu[:].rearrange("p (k c) -> p k c", k=16)
                    for fc in range(2):
                        f = 2 * b + fc
                        pg, pgb = self.bank()
                        S.sync("pe", reads=[hb, wgb], writes=[pgb])
                        for k in range(16):
                            ins = nc.tensor.matmul(pg[:, 0:ntok], lhsT=wg3[:, k, fc * 128:(fc + 1) * 128],
                                                   rhs=h[:, k, 0:ntok], start=(k == 0), stop=(k == 15))
                        S.done("pe", ins, reads=[hb, wgb], writes=[pgb])
                        pu, pub = self.bank()
                        S.sync("pe", reads=[hb, wub], writes=[pub])
                        for k in range(16):
                            ins = nc.tensor.matmul(pu[:, 0:ntok], lhsT=wu3[:, k, fc * 128:(fc + 1) * 128],
                                                   rhs=h[:, k, 0:ntok], start=(k == 0), stop=(k == 15))
                        S.done("pe", ins, reads=[hb, wub], writes=[pub])
                        sg, sgb = sgs[sgn % 3]
                        sgn += 1
                        S.op("act", lambda e: e.activation(out=sg[:, 0:ntok], in_=pg[:, 0:ntok], func=AF.Silu),
                             reads=[pgb], writes=[sgb])
                        S.op("dve", lambda e: e.tensor_tensor(out=actb[:, f, 0:ntok], in0=pu[:, 0:ntok], in1=sg[:, 0:ntok],
                                                              op=ALU.mult), reads=[pub, sgb], writes=[actbuf[f]])
                ep.begin(tt, ntok, src_s)
                for db in range(16):
                    wd, wdb = ds.get(i * 16 + db)
                    wd3 = wd[:].rearrange("p (f c) -> p f c", f=FC)
                    pt, pb = self.bank()
                    S.sync("pe", reads=actbuf + [wdb], writes=[pb])
                    for f in range(FC):
                        ins = nc.tensor.matmul(pt[:, 0:ntok], lhsT=wd3[:, f, :], rhs=actb[:, f, 0:ntok],
                                               start=(f == 0), stop=(f == FC - 1))
                    S.done("pe", ins, reads=actbuf + [wdb], writes=[pb])
                    ep.chunk(db, pt, pb)
                ep.finish()
            self.nbanks = 8
            S.barrier()

    def phase_m1a(self):
        nc, S, W = self.nc, self.S, self.W
        with ExitStack() as st:
            wt = self.sb(st, "m1w", [128, 16, 2, 512], BF16)
            wb = Buf("m1w")
            for b in range(2):
                S.dma(wt[:, :, b, :], W["win"][2 + b].rearrange("p (k c) -> p k c", k=16), reads=[self.Wb[("win", 2 + b)]],
                      writes=[wb])
            dftc = self.sb(st, "m1dftc", [128, 2, 512], BF16)
            S.dma(dftc[:], self.C["dft_c"][:, :, :], writes=[wb])
            hin = [(self.sb(st, self.name("m1h"), [128, 16, 512], BF16), Buf("m1h")) for i in range(2)]
            ufs = [(self.sb(st, self.name("m1u"), [128, 8, 512], BF16), Buf("m1u")) for i in range(2)]
            pqs = [(self.sb(st, self.name("m1pq"), [128, 4, 512], BF16), Buf("m1pq")) for i in range(3)]
            loads = []
            for tt, (s, j, ntok) in enumerate(self.tiles):
                loads.append((lambda tt, ntok: (lambda t: [(t[:, :, 0:ntok], self.tile_ap(self.HT[0], tt, ntok),
                                                            [self.HTb[0][tt]])]))(tt, ntok))
            hs = Stream(S, hin, loads)
            pqn = 0
            for tt, (s, j, ntok) in enumerate(self.tiles):
                h, hb = hs.get(tt)
                uf, ub = ufs[tt % 2]
                for mc in range(8):
                    pt, pb = self.bank()
                    S.sync("pe", reads=[hb, wb], writes=[pb])
                    for k in range(16):
                        ins = nc.tensor.matmul(pt[:, 0:ntok], lhsT=wt[:, k, mc // 4, (mc % 4) * 128:(mc % 4 + 1) * 128],
                                               rhs=h[:, k, 0:ntok], start=(k == 0), stop=(k == 15))
                    S.done("pe", ins, reads=[hb, wb], writes=[pb])
                    S.op("act", lambda e: e.copy(out=uf[:, mc, 0:ntok], in_=pt[:, 0:ntok]), reads=[pb], writes=[ub])
                for sub in range(ntok // 128):
                    pq, pqb = pqs[pqn % 3]
                    pqn += 1
                    for g in range(4):
                        pt, pb = self.bank()
                        S.sync("pe", reads=[ub, wb], writes=[pb])
                        for jj in range(2):
                            ins = nc.tensor.matmul(pt[:, :], lhsT=uf[:, 2 * g + jj, sub * 128:(sub + 1) * 128],
                                                   rhs=dftc[:, jj, :], start=(jj == 0), stop=(jj == 1))
                        S.done("pe", ins, reads=[ub, wb], writes=[pb])
                        S.op("dve", lambda e: e.tensor_copy(out=pq[:, g, :], in_=pt[:, :]), reads=[pb], writes=[pqb])
                    ch = (j * 4 + sub) if j < 4 else (16 + sub)
                    S.dma(self.PQ[s, ch].rearrange("p (g c) -> p g c", g=4), pq[:], reads=[pqb], writes=[self.PQb[s][ch]])
            S.barrier()

    def phase_m1b(self):
        nc, S, W = self.nc, self.S, self.W
        for pg in range(2):
            with ExitStack() as st:
                wt = self.sb(st, self.name("pbw"), [128, 16, 512], BF16)
                wb = Buf("pbw")
                S.dma(wt[:], W["win"][pg].rearrange("p (k c) -> p k c", k=16), reads=[self.Wb[("win", pg)]], writes=[wb])
                plin = self.sb(st, self.name("pbl"), [128, 2, 2, 256], BF16)
                for gl in range(2):
                    S.dma(plin[:, gl, :, :], W["plin"][2 * pg + gl].rearrange("p (c d) -> p c d", c=2),
                          reads=[self.Wb[("plin", 2 * pg + gl)]], writes=[wb])
                PW = NTOK + 16
                up = self.sb(st, self.name("pbu"), [128, 4, PW], F32)
                upb = Buf("pbu")
                ta = self.sb(st, self.name("pba"), [128, 2, PW], F32)
                tb_ = self.sb(st, self.name("pbb"), [128, 2, PW], F32)
                tab = Buf("pbab")
                pooled = self.sb(st, self.name("pbp"), [128, 4, NTOK], BF16)
                pob = Buf("pbp")
                efac = self.sb(st, self.name("pbe"), [128, 2, 4, 16], F32)
                S.dma(efac[:, 0, :, :], self.C["efac_n"].rearrange("(o w) e -> o w e", o=1).broadcast_to([128, 4, 16]),
                      writes=[wb])
                S.dma(efac[:, 1, :, :], self.C["efac_x"].rearrange("(o w) e -> o w e", o=1).broadcast_to([128, 4, 16]),
                      writes=[wb])
                hin = [(self.sb(st, self.name("pbh"), [128, 16, 512], BF16), Buf("pbh")) for i in range(2)]
                yts = [(self.sb(st, self.name("pby"), [128, 4, 512], BF16), Buf("pby")) for i in range(2)]
                loads = []
                for tt, (s, j, ntok) in enumerate(self.tiles):
                    loads.append((lambda tt, ntok: (lambda t: [(t[:, :, 0:ntok], self.tile_ap(self.HT[0], tt, ntok),
                                                                [self.HTb[0][tt]])]))(tt, ntok))
                hs = Stream(S, hin, loads)
                ytn = 0
                for s in range(NS):
                    for kind in range(2):
                        N = NTOK if kind == 0 else NCTX
                        tids = [s * 5 + j for j in range(4)] if kind == 0 else [s * 5 + 4]
                        S.op("pool", lambda e: e.memset(up[:, :, 0:8], 0.0), reads=[upb], writes=[upb])
                        S.op("pool", lambda e: e.memset(up[:, :, 8 + N:16 + N], 0.0), reads=[upb], writes=[upb])
                        for tt in tids:
                            _, j, ntok = self.tiles[tt]
                            h, hb = hs.get(tt)
                            t0 = 8 + (j * 512 if kind == 0 else 0)
                            for mc in range(4):
                                pt, pb = self.bank()
                                S.sync("pe", reads=[hb, wb], writes=[pb])
                                for k in range(16):
                                    ins = nc.tensor.matmul(pt[:, 0:ntok], lhsT=wt[:, k, mc * 128:(mc + 1) * 128],
                                                           rhs=h[:, k, 0:ntok], start=(k == 0), stop=(k == 15))
                                S.done("pe", ins, reads=[hb, wb], writes=[pb])
                                S.op("act", lambda e: e.copy(out=up[:, mc, t0:t0 + ntok], in_=pt[:, 0:ntok]), reads=[pb],
                                     writes=[upb])
                        PWn = N + 16
                        for gl in range(2):
                            g = 2 * pg + gl
                            U = up[:, 2 * gl:2 * gl + 2, :]
                            eng = "dve" if gl == 0 else "pool"
                            S.op(eng, lambda e: e.tensor_tensor(out=ta[:, :, 1:PWn], in0=U[:, :, 0:PWn - 1], in1=U[:, :, 1:PWn],
                                                                op=ALU.add), reads=[upb, tab], writes=[tab])
                            R = ta
                            if g >= 1:
                                S.op(eng, lambda e: e.tensor_tensor(out=tb_[:, :, 2:PWn - 1], in0=ta[:, :, 1:PWn - 2],
                                                                    in1=ta[:, :, 3:PWn], op=ALU.add), reads=[tab], writes=[tab])
                                R = tb_
                            if g >= 2:
                                S.op(eng, lambda e: e.tensor_tensor(out=ta[:, :, 4:PWn - 3], in0=tb_[:, :, 2:PWn - 5],
                                                                    in1=tb_[:, :, 6:PWn - 1], op=ALU.add), reads=[tab],
                                     writes=[tab])
                                R = ta
                            if g >= 3:
                                S.op(eng, lambda e: e.tensor_tensor(out=tb_[:, :, 8:PWn - 8], in0=ta[:, :, 4:PWn - 12],
                                                                    in1=ta[:, :, 12:PWn - 4], op=ALU.add), reads=[tab],
                                     writes=[tab])
                                R = tb_
                            w = POOL_WINDOWS[g]
                            for side in range(2):
                                c0 = 8 if side == 0 else N
                                S.op(eng, lambda e: e.tensor_tensor(
                                    out=R[:, :, c0:c0 + 8], in0=R[:, :, c0:c0 + 8],
                                    in1=efac[:, kind, g, side * 8:(side + 1) * 8].unsqueeze(1).broadcast_to([128, 2, 8]),
                                    op=ALU.mult), reads=[tab, wb], writes=[tab])
                            S.op("dve", lambda e: e.scalar_tensor_tensor(out=pooled[:, 2 * gl:2 * gl + 2, 0:N], in0=R[:, :, 8:8 + N],
                                                                       scalar=1.0 / w, in1=U[:, :, 8:8 + N], op0=ALU.mult,
                                                                       op1=ALU.subtract), reads=[tab, upb, pob], writes=[pob])
                        for tt in tids:
                            _, j, ntok = self.tiles[tt]
                            k0 = (j * 512 if kind == 0 else 0)
                            yt, ytb = yts[ytn % 2]
                            ytn += 1
                            for gl in range(2):
                                g = 2 * pg + gl
                                for dc in range(2):
                                    pt, pb = self.bank()
                                    S.sync("pe", reads=[pob, wb], writes=[pb])
                                    for cc in range(2):
                                        ins = nc.tensor.matmul(pt[:, 0:ntok], lhsT=plin[:, gl, cc, dc * 128:(dc + 1) * 128],
                                                               rhs=pooled[:, 2 * gl + cc, k0:k0 + ntok], start=(cc == 0),
                                                               stop=(cc == 1))
                                    S.done("pe", ins, reads=[pob, wb], writes=[pb])
                                    ch = 2 * g + dc
                                    S.op("act", lambda e: e.mul(out=yt[:, 2 * gl + dc, 0:ntok], in_=pt[:, 0:ntok],
                                                                mul=self.vecT[:, 8, ch:ch + 1]), reads=[pb, self.gb],
                                         writes=[ytb])
                            S.dma(self.tile_ap(self.YT[0], tt, ntok)[:, 4 * pg:4 * pg + 4, :], yt[:, :, 0:ntok], reads=[ytb],
                                  writes=[self.YTb[0][tt]])
                S.barrier()

    def phase_m3(self):
        nc, S, W = self.nc, self.S, self.W
        with ExitStack() as st:
            flin = self.sb(st, "m3l", [128, 4, 2, 256], BF16)
            wb = Buf("m3l")
            for g in range(4):
                S.dma(flin[:, g, :, :], W["flin"][g].rearrange("p (c d) -> p c d", c=2), reads=[self.Wb[("flin", g)]],
                      writes=[wb])
            pq = self.sb(st, "m3pq", [128, 16, 4, 512], BF16)
            pqb = Buf("m3pq")
            tabs = [(self.sb(st, self.name("m3t"), [128, 16, 2, 512], BF16), Buf("m3t")) for i in range(2)]
            fts = [(self.sb(st, self.name("m3f"), [128, 8, 512], BF16), Buf("m3f")) for i in range(2)]
            yts = [(self.sb(st, self.name("m3y"), [128, 8, 512], BF16), Buf("m3y")) for i in range(2)]
            n = 0
            for s in range(NS):
                for kind in range(2):
                    nch = 16 if kind == 0 else 2
                    kt = 512 if kind == 0 else 256
                    tids = [s * 5 + j for j in range(4)] if kind == 0 else [s * 5 + 4]
                    for ch in range(nch):
                        chd = ch if kind == 0 else 16 + ch
                        S.dma(pq[:, ch, :, :], self.PQ[s, chd].rearrange("p (g c) -> p g c", g=4), reads=[self.PQb[s][chd]],
                              writes=[pqb])
                    for ki, tt in enumerate(tids):
                        tab, tbb = tabs[n % 2]
                        ft, ftb = fts[n % 2]
                        yt, ytb = yts[n % 2]
                        n += 1
                        if kind == 0:
                            S.dma(tab[:], self.C["dft_n"][ki].rearrange("p (c t k) -> p c t k", c=16, t=2), writes=[tbb])
                        else:
                            S.dma(tab[:, 0:2, :, 0:256], self.C["dft_x"][0].rearrange("p (c t k) -> p c t k", c=2, t=2),
                                  writes=[tbb])
                        for g in range(4):
                            for mh in range(2):
                                pt, pb = self.bank()
                                S.sync("pe", reads=[pqb, tbb], writes=[pb])
                                for c in range(nch):
                                    nc.tensor.matmul(pt[:, 0:kt], lhsT=pq[:, c, g, mh * 128:(mh + 1) * 128],
                                                     rhs=tab[:, c, 0, 0:kt], start=(c == 0), stop=False)
                                    ins = nc.tensor.matmul(pt[:, 0:kt], lhsT=pq[:, c, g, 256 + mh * 128:256 + (mh + 1) * 128],
                                                           rhs=tab[:, c, 1, 0:kt], start=False, stop=(c == nch - 1))
                                S.done("pe", ins, reads=[pqb, tbb], writes=[pb])
                                S.op("act" if mh == 0 else "dve",
                                     (lambda e: e.copy(out=ft[:, 2 * g + mh, 0:kt], in_=pt[:, 0:kt])) if mh == 0 else
                                     (lambda e: e.tensor_copy(out=ft[:, 2 * g + mh, 0:kt], in_=pt[:, 0:kt])),
                                     reads=[pb], writes=[ftb])
                        for g in range(4):
                            for dc in range(2):
                                pt, pb = self.bank()
                                S.sync("pe", reads=[ftb, wb], writes=[pb])
                                for mh in range(2):
                                    ins = nc.tensor.matmul(pt[:, 0:kt], lhsT=flin[:, g, mh, dc * 128:(dc + 1) * 128],
                                                           rhs=ft[:, 2 * g + mh, 0:kt], start=(mh == 0), stop=(mh == 1))
                                S.done("pe", ins, reads=[ftb, wb], writes=[pb])
                                S.op("act" if dc == 0 else "dve",
                                     (lambda e: e.copy(out=yt[:, 2 * g + dc, 0:kt], in_=pt[:, 0:kt])) if dc == 0 else
                                     (lambda e: e.tensor_copy(out=yt[:, 2 * g + dc, 0:kt], in_=pt[:, 0:kt])),
                                     reads=[pb], writes=[ytb])
                        S.dma(self.tile_ap(self.YT[0], tt, kt)[:, 8:16, :], yt[:, :, 0:kt], reads=[ytb],
                              writes=[self.YTb[0][tt]])
            S.barrier()
        self.dump("yt0", self.YT[0], self.YTb[0], BF16)

    def phase_attn(self):
        nc, S, W, I = self.nc, self.S, self.W, self.I
        with ExitStack() as st:
            ws = [(self.sb(st, self.name("aw"), [128, 16 * 256], BF16), Buf("aw")) for i in range(6)]
            hin = [(self.sb(st, self.name("ah"), [128, 16, 512], BF16), Buf("ah")) for i in range(2)]
            QT = self.sb(st, "aQT", [128, 2, NTOK], BF16)
            KT = self.sb(st, "aKT", [128, 2, NTOK + NCTX], BF16)
            VA = self.sb(st, "aVA", [128, 18, 264], BF16)
            qb_, kb_, vb_ = Buf("aQT"), Buf("aKT"), Buf("aVA")
            cosT = self.sb(st, "acos", [128, 16, 128], F32)
            sinT = self.sb(st, "asin", [128, 16, 128], F32)
            cb = Buf("aconst")
            S.dma(cosT[:], self.C["rope_cos"].rearrange("(j p) d -> p j d", p=128), writes=[cb])
            S.dma(sinT[:], self.C["rope_sin"].rearrange("(j p) d -> p j d", p=128), writes=[cb])
            S.op("dve", lambda e: e.memset(VA[:, :, 256:264], 0.0), writes=[vb_])
            S.op("dve", lambda e: e.memset(VA[:, :, 256:257], 1.0), writes=[vb_])
            lv = self.sb(st, "alv", [128, 4, 128], F32)
            for i, nm in enumerate(("da_lam_q1", "da_lam_k1", "da_lam_q2", "da_lam_k2")):
                S.dma(lv[:, i, :], I[nm][0:1, :].broadcast_to([128, 128]), writes=[cb])
            lsc = self.sb(st, "alsc", [128, 8], F32)
            S.op("dve", lambda e: e.tensor_tensor(out=lv[:, 0, :], in0=lv[:, 0, :], in1=lv[:, 1, :], op=ALU.mult),
                 reads=[cb], writes=[cb])
            S.op("dve", lambda e: e.tensor_tensor(out=lv[:, 2, :], in0=lv[:, 2, :], in1=lv[:, 3, :], op=ALU.mult),
                 reads=[cb], writes=[cb])
            S.op("dve", lambda e: e.reduce_sum(out=lsc[:, 0:1], in_=lv[:, 0, :], axis=AX.X), reads=[cb], writes=[cb])
            S.op("dve", lambda e: e.reduce_sum(out=lsc[:, 1:2], in_=lv[:, 2, :], axis=AX.X), reads=[cb], writes=[cb])
            S.op("act", lambda e: e.activation(out=lsc[:, 2:4], in_=lsc[:, 0:2], func=AF.Exp), reads=[cb], writes=[cb])
            S.op("dve", lambda e: e.scalar_tensor_tensor(out=lsc[:, 4:5], in0=lsc[:, 3:4], scalar=-LAMBDA_INIT1,
                                                         in1=lsc[:, 2:3], op0=ALU.add, op1=ALU.subtract),
                 reads=[cb], writes=[cb])
            Gs = self.sb(st, "aGs", [128, 256], F32)
            S.dma(Gs[:], I["da_subln_g"][0:1, :].broadcast_to([128, 256]), writes=[cb])
            S.op("dve", lambda e: e.tensor_scalar(out=Gs[:], in0=Gs[:], scalar1=(1.0 - LAMBDA_INIT1), scalar2=None,
                                                  op0=ALU.mult), reads=[cb], writes=[cb])
            ssq = self.sb(st, "assq", [128, 18, 4], F32)
            ssb = Buf("assq")
            mm = self.sb(st, "amm", [128, 8], F32)
            negM = self.sb(st, "anegM", [128, 1], F32)
            mb = Buf("amm")
            xs = [(self.sb(st, self.name("axs"), [128, 768], F32), Buf("axs")) for i in range(2)]
            ra = [(self.sb(st, self.name("ara"), [128, 512], F32), Buf("ara")) for i in range(2)]
            rb = [(self.sb(st, self.name("arb"), [128, 512], F32), Buf("arb")) for i in range(2)]
            qk = [(self.sb(st, self.name("aqk"), [128, 512], BF16), Buf("aqk")) for i in range(2)]
            ET = [self.sb(st, "aET%d" % m, [128, 18, 512], BF16) for m in range(2)]
            etb = [[Buf("aET%d_%d" % (m, kc)) for kc in range(18)] for m in range(2)]
            t0s = [(self.sb(st, self.name("at0"), [128, 256], F32), Buf("at0")) for i in range(2)]
            os_ = [(self.sb(st, self.name("aos"), [128, 256], F32), Buf("aos")) for i in range(2)]
            sc_ = [(self.sb(st, self.name("asc"), [128, 8], F32), Buf("asc")) for i in range(2)]
            ab = [(self.sb(st, self.name("aab"), [128, 256], BF16), Buf("aab")) for i in range(2)]
            att = [(self.sb(st, self.name("aatt"), [128, 2, 512], BF16), Buf("aatt")) for i in range(2)]
            wloads, hloads = [], []
            for s in range(NS):
                for h in range(8):
                    for part in range(3):
                        wloads.append(self.wload(("wa", h, part), W["wa"][h, part]))
                    for j in range(5):
                        tt = s * 5 + j
                        ntok = self.tiles[tt][2]
                        hloads.append((lambda tt, ntok: (lambda t: [(t[:, :, 0:ntok], self.tile_ap(self.HT[2], tt, ntok),
                                                                    [self.HTb[2][tt]])]))(tt, ntok))
            wstr = Stream(S, ws, wloads, ahead=3)
            hstr = Stream(S, hin, hloads)
            n_x = 0
            n_att = 0
            n_pv = 0
            for s in range(NS):
                for h in range(8):
                    ih = s * 8 + h
                    wq, wqb = wstr.get(ih * 3 + 0)
                    wk, wkb = wstr.get(ih * 3 + 1)
                    wv, wvb = wstr.get(ih * 3 + 2)
                    w3 = [w[:].rearrange("p (k c) -> p k c", k=16) for w in (wq, wk, wv)]
                    S.op("pool", lambda e: e.memset(ssq[:], 0.0), reads=[ssb], writes=[ssb])
                    for j in range(5):
                        tt = s * 5 + j
                        ntok = self.tiles[tt][2]
                        hT, hb = hstr.get(ih * 5 + j)
                        lat = j < 4
                        for sub in range(ntok // 128):
                            ch = (j * 4 + sub) if lat else (16 + sub)
                            x_, xb_ = xs[n_x % 2]
                            r_a, rab = ra[n_x % 2]
                            r_b, rbb = rb[n_x % 2]
                            q_k, qkb = qk[n_x % 2]
                            n_x += 1
                            parts = ([0, 1, 2] if lat else [1, 2])
                            for part in parts:
                                pt, pb = self.bank()
                                wbuf = (wqb, wkb, wvb)[part]
                                S.sync("pe", reads=[hb, wbuf], writes=[pb])
                                for k in range(16):
                                    ins = nc.tensor.matmul(pt[:, 0:256], lhsT=hT[:, k, sub * 128:(sub + 1) * 128],
                                                           rhs=w3[part][:, k, :], start=(k == 0), stop=(k == 15))
                                S.done("pe", ins, reads=[hb, wbuf], writes=[pb])
                                if part == 2:
                                    S.op("act", lambda e: e.copy(out=VA[:, ch, 0:256], in_=pt[:, 0:256]), reads=[pb],
                                         writes=[vb_])
                                else:
                                    S.op("act", lambda e: e.copy(out=x_[:, part * 256:(part + 1) * 256], in_=pt[:, 0:256]),
                                         reads=[pb], writes=[xb_])
                            c0 = 0 if lat else 256
                            nb = 4 if lat else 2
                            W_ = nb * 128
                            xv = x_[:, c0:c0 + W_]
                            if lat:
                                x5 = xv.rearrange("p (b r h d) -> p b r h d", b=nb, r=2, h=2)
                                a5 = r_a[:, 0:W_].rearrange("p (b r h d) -> p b r h d", b=nb, r=2, h=2)
                                b5 = r_b[:, 0:W_].rearrange("p (b r h d) -> p b r h d", b=nb, r=2, h=2)
                                cos4 = cosT[:, ch, :].unsqueeze(1).broadcast_to([128, nb, 128])
                                sin5 = sinT[:, ch, :].rearrange("p (r h d) -> p r h d", r=2, h=2)
                                S.op("dve", lambda e: e.tensor_tensor(out=r_a[:, 0:W_].rearrange("p (b d) -> p b d", b=nb),
                                                                      in0=xv.rearrange("p (b d) -> p b d", b=nb), in1=cos4,
                                                                      op=ALU.mult), reads=[xb_, cb], writes=[rab])
                                for hh in range(2):
                                    S.op("pool", lambda e: e.tensor_tensor(
                                        out=b5[:, :, :, hh, :], in0=x5[:, :, :, 1 - hh, :],
                                        in1=sin5[:, :, hh, :].unsqueeze(1).broadcast_to([128, nb, 2, 32]), op=ALU.mult),
                                        reads=[xb_, cb], writes=[rbb])
                                S.op("dve", lambda e: e.tensor_tensor(out=r_a[:, 0:W_], in0=r_a[:, 0:W_], in1=r_b[:, 0:W_],
                                                                      op=ALU.add), reads=[rab, rbb], writes=[rab])
                                src, srcb = r_a[:, 0:W_], rab
                            else:
                                src, srcb = xv, xb_
                            S.op("act", lambda e: e.copy(out=q_k[:, 0:W_], in_=src), reads=[srcb], writes=[qkb])
                            sqd = r_b[:, 0:W_]
                            S.op("dve", lambda e: e.tensor_tensor(out=sqd, in0=src, in1=src, op=ALU.mult), reads=[srcb, rbb],
                                 writes=[rbb])
                            S.op("dve", lambda e: e.reduce_sum(out=ssq[:, ch, (4 - nb):4],
                                                               in_=sqd.rearrange("p (b d) -> p b d", b=nb), axis=AX.X),
                                 reads=[rbb, ssb], writes=[ssb])
                            pt, pb = self.bank()
                            ptb = pt[:].bitcast(BF16)
                            S.sync("pe", reads=[qkb, self.gb], writes=[pb])
                            for bi in range(nb):
                                ins = nc.tensor.transpose(out=ptb[:, bi * 128:(bi + 1) * 128], in_=q_k[:, bi * 128:(bi + 1) * 128],
                                                          identity=self.identb[:])
                            S.done("pe", ins, reads=[qkb, self.gb], writes=[pb])
                            tk0 = ch * 128
                            if lat:
                                S.op("dve", lambda e: e.tensor_copy(out=QT[:, :, tk0:tk0 + 128],
                                                                    in_=ptb[:, 0:256].rearrange("p (m t) -> p m t", m=2)),
                                     reads=[pb, qb_], writes=[qb_])
                                S.op("act", lambda e: e.copy(out=KT[:, :, tk0:tk0 + 128],
                                                             in_=ptb[:, 256:512].rearrange("p (m t) -> p m t", m=2)),
                                     reads=[pb, kb_], writes=[kb_])
                            else:
                                S.op("act", lambda e: e.copy(out=KT[:, :, tk0:tk0 + 128],
                                                             in_=ptb[:, 0:256].rearrange("p (m t) -> p m t", m=2)),
                                     reads=[pb, kb_], writes=[kb_])
                    S.op("dve", lambda e: e.reduce_max(out=mm[:, 0:1], in_=ssq[:, :, 0:2], axis=AX.XY), reads=[ssb, mb],
                         writes=[mb])
                    S.op("dve", lambda e: e.reduce_max(out=mm[:, 1:2], in_=ssq[:, :, 2:4], axis=AX.XY), reads=[ssb, mb],
                         writes=[mb])
                    pt, pb = self.bank()
                    S.sync("pe", reads=[mb, self.gb], writes=[pb])
                    ins = nc.tensor.transpose(out=pt[0:2, 0:128], in_=mm[:, 0:2], identity=self.ident[:])
                    S.done("pe", ins, reads=[mb, self.gb], writes=[pb])
                    S.op("dve", lambda e: e.reduce_max(out=mm[0:2, 2:3], in_=pt[0:2, 0:128], axis=AX.X), reads=[pb, mb],
                         writes=[mb])
                    S.op("dve", lambda e: e.tensor_scalar(out=mm[0:2, 4:6], in0=self.ident[0:2, 0:2], scalar1=mm[0:2, 2:3],
                                                          scalar2=None, op0=ALU.mult), reads=[mb, self.gb], writes=[mb])
                    pt, pb = self.bank()
                    S.sync("pe", reads=[mb, self.gb], writes=[pb])
                    ins = nc.tensor.matmul(pt[:, 0:2], lhsT=self.ones_f[0:2, :], rhs=mm[0:2, 4:6], start=True, stop=True)
                    S.done("pe", ins, reads=[mb, self.gb], writes=[pb])
                    S.op("dve", lambda e: e.tensor_copy(out=mm[:, 6:8], in_=pt[:, 0:2]), reads=[pb, mb], writes=[mb])
                    S.op("dve", lambda e: e.tensor_tensor(out=mm[:, 3:4], in0=mm[:, 6:7], in1=mm[:, 7:8], op=ALU.mult),
                         reads=[mb], writes=[mb])
                    S.op("act", lambda e: e.activation(out=mm[:, 3:4], in_=mm[:, 3:4], func=AF.Sqrt), reads=[mb], writes=[mb])
                    S.op("dve", lambda e: e.tensor_scalar(out=negM[:], in0=mm[:, 3:4], scalar1=-SM_SCALE * float(D), scalar2=None,
                                                          op0=ALU.mult), reads=[mb], writes=[mb])
                    for qt in range(4):
                        tt = s * 5 + qt
                        for m in range(2):
                            for kc in range(18):
                                pt, pb = self.bank()
                                S.sync("pe", reads=[qb_, kb_], writes=[pb])
                                ins = nc.tensor.matmul(pt[:, :], lhsT=KT[:, m, kc * 128:(kc + 1) * 128],
                                                       rhs=QT[:, m, qt * 512:(qt + 1) * 512], start=True, stop=True)
                                S.done("pe", ins, reads=[qb_, kb_], writes=[pb])
                                S.op("act", lambda e: e.activation(out=ET[m][:, kc, :], in_=pt[:, :], func=AF.Exp,
                                                                   bias=negM[:, 0:1], scale=SM_SCALE), reads=[pb, mb],
                                     writes=[etb[m][kc]])
                        at_, atb = att[n_att % 2]
                        n_att += 1
                        for qs in range(4):
                            pv = []
                            for m in range(2):
                                pt, pb = self.bank()
                                S.sync("pe", reads=etb[m] + [vb_], writes=[pb])
                                for kc in range(18):
                                    ins = nc.tensor.matmul(pt[:, 0:258], lhsT=ET[m][:, kc, qs * 128:(qs + 1) * 128],
                                                           rhs=VA[:, kc, 0:258], start=(kc == 0), stop=(kc == 17))
                                S.done("pe", ins, reads=etb[m] + [vb_], writes=[pb])
                                pv.append((pt, pb))
                            t0, t0b = t0s[n_pv % 2]
                            o_, ob_ = os_[n_pv % 2]
                            sc, scb = sc_[n_pv % 2]
                            a_, abb = ab[n_pv % 2]
                            n_pv += 1
                            (p0, p0b), (p1, p1b) = pv
                            S.op("dve", lambda e: e.reciprocal(out=sc[:, 0:1], in_=p0[:, 256:257]), reads=[p0b, scb],
                                 writes=[scb])
                            S.op("dve", lambda e: e.reciprocal(out=sc[:, 1:2], in_=p1[:, 256:257]), reads=[p1b, scb],
                                 writes=[scb])
                            S.op("dve", lambda e: e.tensor_tensor(out=sc[:, 2:3], in0=sc[:, 1:2], in1=lsc[:, 4:5], op=ALU.mult),
                                 reads=[scb, cb], writes=[scb])
                            S.op("act", lambda e: e.mul(out=t0[:], in_=p0[:, 0:256], mul=sc[:, 0:1]), reads=[p0b, scb],
                                 writes=[t0b])
                            S.op("dve", lambda e: e.scalar_tensor_tensor(out=o_[:], in0=p1[:, 0:256], scalar=sc[:, 2:3],
                                                                         in1=t0[:], op0=ALU.mult, op1=ALU.add),
                                 reads=[p1b, scb, t0b], writes=[ob_])
                            S.op("dve", lambda e: e.tensor_tensor(out=t0[:], in0=o_[:], in1=o_[:], op=ALU.mult),
                                 reads=[ob_, t0b], writes=[t0b])
                            S.op("dve", lambda e: e.reduce_sum(out=sc[:, 3:4], in_=t0[:], axis=AX.X), reads=[t0b, scb],
                                 writes=[scb])
                            S.op("dve", lambda e: e.tensor_scalar(out=sc[:, 4:5], in0=sc[:, 3:4], scalar1=1.0 / 256.0,
                                                                  scalar2=RMS_EPS, op0=ALU.mult, op1=ALU.add),
                                 reads=[scb], writes=[scb])
                            S.op("act", lambda e: e.activation(out=sc[:, 4:5], in_=sc[:, 4:5], func=AF.Sqrt), reads=[scb],
                                 writes=[scb])
                            S.op("dve", lambda e: e.reciprocal(out=sc[:, 5:6], in_=sc[:, 4:5]), reads=[scb], writes=[scb])
                            S.op("dve", lambda e: e.scalar_tensor_tensor(out=a_[:], in0=o_[:], scalar=sc[:, 5:6], in1=Gs[:],
                                                                         op0=ALU.mult, op1=ALU.mult),
                                 reads=[ob_, scb, cb], writes=[abb])
                            pt, pb = self.bank()
                            ptb = pt[:].bitcast(BF16)
                            S.sync("pe", reads=[abb, self.gb], writes=[pb])
                            for e2 in range(2):
                                ins = nc.tensor.transpose(out=ptb[:, e2 * 128:(e2 + 1) * 128], in_=a_[:, e2 * 128:(e2 + 1) * 128],
                                                          identity=self.identb[:])
                            S.done("pe", ins, reads=[abb, self.gb], writes=[pb])
                            S.op("act", lambda e: e.copy(out=at_[:, :, qs * 128:(qs + 1) * 128],
                                                         in_=ptb[:, 0:256].rearrange("p (c t) -> p c t", c=2)),
                                 reads=[pb], writes=[atb])
                        S.dma(self.tile_ap(self.YT[1], tt, 512)[:, 2 * h:2 * h + 2, :], at_[:], reads=[atb],
                              writes=[self.YTb[1][tt]])
            S.barrier()
        self.dump("yt1", self.YT[1], self.YTb[1], BF16)

    def dump_stage(self, nm):
        m = {"proj0": (1, 1), "ffn0": (2, 2), "proj1": (3, 3)}
        if nm in m:
            xi, hi = m[nm]
            self.dump("xt%d" % xi, self.XT[xi], self.XTb[xi], F32)
            self.dump("ht%d" % hi, self.HT[hi], self.HTb[hi], BF16)


def build_program(consts):
    mk = MK()
    nc = mk.build(consts)
    return nc, mk


_CACHE = {}


def kernel(**inputs):
    consts = _host_consts()
    n_cores = 8 if not DEBUG else int(os.environ.get("KCORES", "1"))
    nc, mk = build_program(consts)
    x = np.ascontiguousarray(inputs["x"], dtype=np.float32)
    in_maps = []
    for r in range(n_cores):
        m = {}
        for k, v in inputs.items():
            v = np.asarray(v)
            if k == "x":
                m[k] = np.ascontiguousarray(v[r * NS:(r + 1) * NS])
            elif k == "c":
                m[k] = np.ascontiguousarray(v[r * NS:(r + 1) * NS])
            elif k == "ctx":
                m[k] = np.ascontiguousarray(v[r * NS:(r + 1) * NS])
            else:
                m[k] = v
        for k, v in consts.items():
            m["k_" + k] = v
        in_maps.append(m)
    res = run_bass_kernel_spmd(nc, in_maps, core_ids=list(range(n_cores)))
    if DEBUG:
        _CACHE["res"] = res
    out = np.concatenate([np.asarray(r["out"]) for r in res.results], axis=0)
    return out.astype(np.float32)
```
